# Optimizing a Trainium2 kernel written in Bass

```python
import math
import jax, jax.numpy as jnp
from jax import lax
import numpy as np

D_MODEL = 1024
BATCH = 2
SEQ = 8192
DEPTH = 2
DEC_BATCH = 8
DEC_SEQ = 32
PAST_LEN = 2048

CHUNK = 64
N_A = DEPTH // 2
N_B = DEPTH - N_A
POOL_WINDOWS = (2, 4, 8, 16)
POOL_GROUP = D_MODEL // 8
POOL_WIDTH = 4 * POOL_GROUP
POOL_HIST = max(POOL_WINDOWS) - 1
N_DIFF_HEADS = 4
DIFF_HEAD_DIM = D_MODEL // 16
DIFF_V_DIM = 2 * DIFF_HEAD_DIM
DIFF_Q_WIDTH = N_DIFF_HEADS * 2 * DIFF_HEAD_DIM
DIFF_OUT_WIDTH = N_DIFF_HEADS * DIFF_V_DIM
N_MEM = 256
N_MEM_HEADS = 4
MEM_HEAD_DIM = D_MODEL // 8
MEM_WIDTH = N_MEM_HEADS * MEM_HEAD_DIM
MIX_IN = POOL_WIDTH + MEM_WIDTH
MIX_OUT = POOL_WIDTH + MEM_WIDTH
D_FF = 4 * D_MODEL
Q_BLOCK = 128
EPS = 1e-6

kernel_name = "yoco_pool_diffattn_streaming_step"


def rmsnorm(x, g):
    xf = x.astype(jnp.float32)
    y = xf * lax.rsqrt(jnp.mean(xf * xf, axis=-1, keepdims=True) + EPS)
    return (y * g.astype(jnp.float32)).astype(x.dtype)


def sq_relu_mlp(x, w1, w2):
    h = jax.nn.relu(x @ w1)
    return (h * h) @ w2


def lambda_init(layer_idx):
    return 0.8 - 0.6 * math.exp(-0.3 * layer_idx)


def alibi_slopes():
    return 2.0 ** (-8.0 * jnp.arange(1, N_DIFF_HEADS + 1, dtype=jnp.float32) / N_DIFF_HEADS)


def pool_mix(u, hist, pos, w_grp, scale):
    B, T, P = u.shape
    full = jnp.concatenate([hist, u], axis=1).astype(jnp.float32)
    cs = jnp.concatenate([jnp.zeros((B, 1, P), jnp.float32), jnp.cumsum(full, axis=1)], axis=1)
    end = cs[:, POOL_HIST + 1:]
    uf = u.astype(jnp.float32)
    outs = []
    for g, w in enumerate(POOL_WINDOWS):
        sl = slice(g * POOL_GROUP, (g + 1) * POOL_GROUP)
        start = cs[:, POOL_HIST + 1 - w: POOL_HIST + 1 - w + T, sl]
        cnt = jnp.minimum(pos + 1, w).astype(jnp.float32)[None, :, None]
        d = ((end[..., sl] - start) / cnt - uf[..., sl]).astype(u.dtype)
        outs.append(d @ w_grp[g])
    return jnp.concatenate(outs, axis=-1) * scale


def mem_kv(mem, g, w):
    B = mem.shape[0]
    kv = rmsnorm(mem, g) @ w
    k, v = jnp.split(kv, 2, axis=-1)
    return (k.reshape(B, N_MEM, N_MEM_HEADS, MEM_HEAD_DIM),
            v.reshape(B, N_MEM, N_MEM_HEADS, MEM_HEAD_DIM))


def mem_attend(q, k, v):
    B, T = q.shape[:2]
    s = jnp.einsum('bthd,bmhd->bhtm', q, k).astype(jnp.float32) * (MEM_HEAD_DIM ** -0.5)
    p = jax.nn.softmax(s, axis=-1).astype(v.dtype)
    return jnp.einsum('bhtm,bmhd->bthd', p, v).reshape(B, T, MEM_WIDTH)


def diff_core(q, k, v, q_pos, k_pos, lam, slopes):
    s = jnp.einsum('bqhcd,bkhcd->cbhqk', q, k).astype(jnp.float32) * (DIFF_HEAD_DIM ** -0.5)
    dist = jnp.abs(q_pos[:, None] - k_pos[None, :]).astype(jnp.float32)
    bias = -slopes[:, None, None] * dist[None]
    vis = (k_pos[None, :] // CHUNK) <= (q_pos[:, None] // CHUNK)
    s = jnp.where(vis, s + bias, -jnp.inf)
    p = jax.nn.softmax(s, axis=-1)
    a = p[0] - lam * p[1]
    return jnp.einsum('bhqk,bkhe->bqhe', a.astype(v.dtype), v)


def diff_attn_blocks(q, k, v, lam, slopes):
    B, T = q.shape[:2]
    nb = T // Q_BLOCK
    qb = q.reshape(B, nb, Q_BLOCK, N_DIFF_HEADS, 2, DIFF_HEAD_DIM).transpose(1, 0, 2, 3, 4, 5)
    k_pos = jnp.arange(T)

    def one(args):
        qi, i = args
        q_pos = i * Q_BLOCK + jnp.arange(Q_BLOCK)
        return diff_core(qi, k, v, q_pos, k_pos, lam, slopes)

    o = lax.map(one, (qb, jnp.arange(nb)))
    return o.transpose(1, 0, 2, 3, 4).reshape(B, T, N_DIFF_HEADS, DIFF_V_DIM)


def trunk(x, pos, pool_hist, mem_k, mem_v, past_k, past_v, blocked,
          g_attn, w_in, w_out, g_ffn, w_ff1, w_ff2, w_pool, pool_scale,
          lambda_qk, g_subln, g_kv, w_kv, g_final):
    B, T, _ = x.shape
    P0 = past_k.shape[1]
    slopes = alibi_slopes()
    new_pool = []
    new_k = new_v = k_all = v_all = None
    for l in range(DEPTH):
        h = rmsnorm(x, g_attn[l])
        z = h @ w_in[l]
        mix, qm = z[..., :MIX_IN - MEM_WIDTH], z[..., MIX_IN - MEM_WIDTH:]
        m_out = mem_attend(qm.reshape(B, T, N_MEM_HEADS, MEM_HEAD_DIM), mem_k[l], mem_v[l])
        if l < N_A:
            hist = pool_hist[l]
            new_pool.append(jnp.concatenate([hist, mix], axis=1)[:, -POOL_HIST:])
            t_out = pool_mix(mix, hist, pos, w_pool[l], pool_scale[l])
        else:
            b = l - N_A
            if new_k is None:
                kv = rmsnorm(x, g_kv) @ w_kv
                kk, vv = jnp.split(kv, 2, axis=-1)
                new_k = kk.reshape(B, T, N_DIFF_HEADS, DIFF_V_DIM)
                new_v = vv.reshape(B, T, N_DIFF_HEADS, DIFF_V_DIM)
                k_all = jnp.concatenate([past_k, new_k], axis=1).reshape(
                    B, P0 + T, N_DIFF_HEADS, 2, DIFF_HEAD_DIM)
                v_all = jnp.concatenate([past_v, new_v], axis=1)
            lq = lambda_qk[b].astype(jnp.float32)
            lam_i = lambda_init(l)
            lam = jnp.exp(jnp.sum(lq[0] * lq[1])) - jnp.exp(jnp.sum(lq[2] * lq[3])) + lam_i
            q = mix.reshape(B, T, N_DIFF_HEADS, 2, DIFF_HEAD_DIM)
            if blocked:
                o = diff_attn_blocks(q, k_all, v_all, lam, slopes)
            else:
                o = diff_core(q, k_all, v_all, pos, jnp.arange(P0 + T), lam, slopes)
            o = rmsnorm(o, g_subln[b]) * (1.0 - lam_i)
            t_out = o.reshape(B, T, DIFF_OUT_WIDTH)
        x = x + jnp.concatenate([t_out, m_out], axis=-1) @ w_out[l]
        x = x + sq_relu_mlp(rmsnorm(x, g_ffn[l]), w_ff1[l], w_ff2[l])
    return rmsnorm(x, g_final), jnp.stack(new_pool), new_k, new_v


def setup_inputs(seed: int = 0) -> dict:
    key = jax.random.key(seed)
    ks = iter(jax.random.split(key, 32))
    f32 = jnp.float32
    nrm = lambda shape, s=1.0: jax.random.normal(next(ks), shape, f32) * s
    gain = lambda shape: 1.0 + 0.05 * jax.random.normal(next(ks), shape, f32)
    return {
        "x_prompt": nrm((BATCH, SEQ, D_MODEL)),
        "x_sample": nrm((DEC_BATCH, DEC_SEQ, D_MODEL)),
        "mem_prompt": nrm((BATCH, N_MEM, D_MODEL)),
        "cache_k": nrm((DEC_BATCH, PAST_LEN, N_DIFF_HEADS, DIFF_V_DIM)),
        "cache_v": nrm((DEC_BATCH, PAST_LEN, N_DIFF_HEADS, DIFF_V_DIM)),
        "cache_mem_k": nrm((DEPTH, DEC_BATCH, N_MEM, N_MEM_HEADS, MEM_HEAD_DIM)),
        "cache_mem_v": nrm((DEPTH, DEC_BATCH, N_MEM, N_MEM_HEADS, MEM_HEAD_DIM)),
        "state_pool": nrm((N_A, DEC_BATCH, POOL_HIST, POOL_WIDTH)),
        "g_attn": gain((DEPTH, D_MODEL)),
        "w_in": nrm((DEPTH, D_MODEL, MIX_IN), D_MODEL ** -0.5),
        "w_out": nrm((DEPTH, MIX_OUT, D_MODEL), MIX_OUT ** -0.5),
        "g_mem": gain((DEPTH, D_MODEL)),
        "w_mem_kv": nrm((DEPTH, D_MODEL, 2 * MEM_WIDTH), D_MODEL ** -0.5),
        "g_ffn": gain((DEPTH, D_MODEL)),
        "w_ff1": nrm((DEPTH, D_MODEL, D_FF), D_MODEL ** -0.5),
        "w_ff2": nrm((DEPTH, D_FF, D_MODEL), D_FF ** -0.5),
        "w_pool": nrm((N_A, 4, POOL_GROUP, POOL_GROUP), POOL_GROUP ** -0.5),
        "pool_scale": gain((N_A, POOL_WIDTH)),
        "lambda_qk": nrm((N_B, 4, DIFF_HEAD_DIM), 0.1),
        "g_subln": gain((N_B, DIFF_V_DIM)),
        "g_kv": gain((D_MODEL,)),
        "w_kv": nrm((D_MODEL, 2 * N_DIFF_HEADS * DIFF_V_DIM), D_MODEL ** -0.5),
        "g_final": gain((D_MODEL,)),
    }


def reference(x_prompt, x_sample, mem_prompt, cache_k, cache_v, cache_mem_k, cache_mem_v, state_pool,
              g_attn, w_in, w_out, g_mem, w_mem_kv, g_ffn, w_ff1, w_ff2, w_pool, pool_scale,
              lambda_qk, g_subln, g_kv, w_kv, g_final):
    weights = (g_attn, w_in, w_out, g_ffn, w_ff1, w_ff2, w_pool, pool_scale,
               lambda_qk, g_subln, g_kv, w_kv, g_final)
    B, T, _ = x_prompt.shape
    mks, mvs = [], []
    for l in range(DEPTH):
        mk, mv = mem_kv(mem_prompt, g_mem[l], w_mem_kv[l])
        mks.append(mk)
        mvs.append(mv)
    mem_k_prompt = jnp.stack(mks)
    mem_v_prompt = jnp.stack(mvs)
    hist0 = jnp.zeros((N_A, B, POOL_HIST, POOL_WIDTH), x_prompt.dtype)
    past0 = jnp.zeros((B, 0, N_DIFF_HEADS, DIFF_V_DIM), x_prompt.dtype)
    y_prompt, pool_prompt, k_prompt, v_prompt = trunk(
        x_prompt, jnp.arange(T), hist0, mem_k_prompt, mem_v_prompt, past0, past0, True, *weights)
    P0 = cache_k.shape[1]
    Ts = x_sample.shape[1]
    y_sample, pool_sample, k_sample, v_sample = trunk(
        x_sample, P0 + jnp.arange(Ts), state_pool, cache_mem_k, cache_mem_v, cache_k, cache_v, False, *weights)
    return (y_prompt, y_sample, mem_k_prompt, mem_v_prompt, pool_prompt, k_prompt, v_prompt,
            pool_sample, k_sample, v_sample)
```

```python
import math
import contextlib
import numpy as np
import concourse.bass as bass
import concourse.mybir as mybir
from concourse.bass_utils import run_bass_kernel_spmd

DT = mybir.dt
F32 = DT.float32
BF16 = DT.bfloat16
ALU = mybir.AluOpType
ACTF = mybir.ActivationFunctionType
ENGS = ("pe", "act", "dve", "pool", "sp")

D = 1024
NP = 2048
NSM = 32
NT = NP + NSM
NH = 256
SEQ = 8192
PAST = 2048
EPS = 1e-6
LAM_I = 0.8 - 0.6 * math.exp(-0.3 * 1)
SLOPES = [2.0 ** (-8.0 * (h + 1) / 4) for h in range(4)]
NEG = -30000.0
TILES = [(0, 512), (512, 512), (1024, 512), (1536, 512), (2048, 32)]
NKBUF = 4


def _prod(xs):
    r = 1
    for x in xs:
        r *= int(x)
    return r


class Op:
    __slots__ = ("idx", "eng", "fn", "kind", "deps", "flag", "mile", "dsem", "dval", "dprev")

    def __init__(self, idx, eng, fn, kind):
        self.idx = idx
        self.eng = eng
        self.fn = fn
        self.kind = kind
        self.deps = set()
        self.flag = False
        self.mile = 0
        self.dsem = None
        self.dval = 0
        self.dprev = 0


class Sched:
    def __init__(self, nc):
        self.nc = nc
        self.ops = []
        self.sb_top = 16512
        self.sb_limit = 229344
        self.tinfo = {}
        self.wr = {}
        self.rd = {}
        self.n_dma_sems = {"sp": 48, "pool": 32, "act": 8}

    def sbuf(self, name, shape, dtype, at=None):
        dsize = DT.size(dtype)
        per_part = _prod(shape[1:]) * dsize
        if at is None:
            at = (self.sb_top + 63) // 64 * 64
            self.sb_top = at + per_part
        assert at + per_part <= self.sb_limit, (name, at, per_part)
        t = self.nc.alloc_sbuf_tensor_at(name, list(shape), dtype, offset=int(at))
        self.tinfo[t.name] = ("SB", int(at), _prod(shape[1:]), dsize)
        return t

    def reserve(self, nbytes):
        at = (self.sb_top + 63) // 64 * 64
        self.sb_top = at + nbytes
        assert self.sb_top <= self.sb_limit, ("reserve", at, nbytes)
        return at

    def psum(self, name, shape=(128, 512), dtype=F32):
        t = self.nc.alloc_psum_tensor(name, list(shape), dtype)
        self.tinfo[t.name] = ("PS:" + t.name, 0, _prod(shape[1:]), DT.size(dtype))
        return t

    def dram(self, name, shape, dtype, kind="Internal"):
        t = self.nc.dram_tensor(name, list(shape), dtype, kind=kind)
        self.tinfo[t.name] = ("DR:" + t.name, 0, None, DT.size(dtype))
        return t

    def regions(self, ap):
        t = ap.tensor
        key, base, per_part, dsize = self.tinfo[t.name]
        off = int(ap.offset)
        dims = [(int(s), int(c)) for s, c in ap.ap]
        if per_part is None:
            lo = hi = off
            for s, c in dims:
                if c > 1:
                    if s >= 0:
                        hi += s * (c - 1)
                    else:
                        lo += s * (c - 1)
            return [(key, 0, 1, lo, hi + 1)]
        p0 = off // per_part
        inoff = off % per_part
        pstride, pcount = dims[0]
        if pstride == 0 or pcount == 1:
            p1 = p0 + 1
        else:
            assert pstride == per_part, (t.name, dims, per_part)
            p1 = p0 + pcount
        lo = hi = inoff
        for s, c in dims[1:]:
            if c > 1:
                if s >= 0:
                    hi += s * (c - 1)
                else:
                    lo += s * (c - 1)
        if key.startswith("PS:"):
            b0 = (lo * dsize) // 2048
            b1 = (hi * dsize) // 2048
            return [(key + ":%d" % bk, p0 // 32 * 32, (p1 + 31) // 32 * 32, 0, 2048) for bk in range(b0, b1 + 1)]
        return [(key, p0, p1, base + lo * dsize, base + (hi + 1) * dsize)]

    def _add(self, eng, fn, reads, writes, kind="c"):
        op = Op(len(self.ops), eng, fn, kind)
        rregs = [rg for a in reads if a is not None for rg in self.regions(a)]
        wregs = [rg for a in writes if a is not None for rg in self.regions(a)]
        deps = op.deps
        for (key, p0, p1, lo, hi) in rregs:
            for w in self.wr.get(key, ()):
                if w[0] < p1 and p0 < w[1] and w[2] < hi and lo < w[3]:
                    deps.add(w[4])
        for (key, p0, p1, lo, hi) in wregs:
            for w in self.wr.get(key, ()):
                if w[0] < p1 and p0 < w[1] and w[2] < hi and lo < w[3]:
                    deps.add(w[4])
            for r in self.rd.get(key, ()):
                if r[0] < p1 and p0 < r[1] and r[2] < hi and lo < r[3]:
                    deps.add(r[4])
        for (key, p0, p1, lo, hi) in wregs:
            wl = self.wr.setdefault(key, [])
            wl[:] = [w for w in wl if not (p0 <= w[0] and w[1] <= p1 and lo <= w[2] and w[3] <= hi)]
            wl.append([p0, p1, lo, hi, op.idx])
            rl = self.rd.get(key)
            if rl:
                rl[:] = [r for r in rl if not (p0 <= r[0] and r[1] <= p1 and lo <= r[2] and r[3] <= hi)]
        inorder = kind == "c"
        for (key, p0, p1, lo, hi) in rregs:
            rl = self.rd.setdefault(key, [])
            done = False
            if inorder:
                for r in rl:
                    if r[5] == eng and r[0] == p0 and r[1] == p1 and r[2] == lo and r[3] == hi:
                        r[4] = op.idx
                        done = True
                        break
            if not done:
                rl.append([p0, p1, lo, hi, op.idx, eng if inorder else None])
        deps.discard(op.idx)
        self.ops.append(op)
        return op

    def c(self, eng, fn, reads, writes):
        return self._add(eng, fn, reads, writes, "c")

    def dma(self, q, out, in_, **kw):
        def fn(e, out=out, in_=in_, kw=kw):
            return e.dma_start(out=out, in_=in_, **kw)
        return self._add(q, fn, [in_], [out], "d")

    def collective(self, q, fn, reads, writes):
        return self._add(q, fn, reads, writes, "cc")

    def emit(self):
        nc = self.nc
        ops = self.ops
        for op in ops:
            for d in op.deps:
                p = ops[d]
                if p.kind == "c":
                    if p.eng == "pe" and op.eng == "pe" and op.kind == "c":
                        continue
                    p.flag = True
        cnt = {e: 0 for e in ENGS}
        for op in ops:
            if op.kind == "c" and op.flag:
                cnt[op.eng] += 1
                op.mile = cnt[op.eng]
        qcount = {}
        ncc = 0
        for op in ops:
            if op.kind == "d":
                k = qcount.get(op.eng, 0)
                qcount[op.eng] = k + 1
                ns = self.n_dma_sems[op.eng]
                op.dsem = (op.eng, k % ns)
                op.dval = 16 * (k // ns + 1)
                op.dprev = 16 * (k // ns)
            elif op.kind == "cc":
                op.dsem = ("cc", ncc)
                ncc += 1
                op.dval = 1
                op.dprev = 0
        with contextlib.ExitStack() as st:
            esem = {e: st.enter_context(nc.semaphore("s_" + e)) for e in ENGS}
            dsem = {}
            for q, n in self.n_dma_sems.items():
                for i in range(min(n, qcount.get(q, 0))):
                    dsem[(q, i)] = st.enter_context(nc.semaphore("d_%s_%d" % (q, i)))
            for i in range(ncc):
                dsem[("cc", i)] = st.enter_context(nc.semaphore("s_cc%d" % i))
            block = st.enter_context(nc.Block())
            per_eng = {e: [op for op in ops if op.eng == e] for e in ENGS}
            final = {}
            for op in ops:
                if op.kind in ("d", "cc"):
                    final[op.dsem] = max(final.get(op.dsem, 0), op.dval)

            def run(engname, eobj):
                waited = {}
                for op in per_eng[engname]:
                    waits = {}
                    for d in op.deps:
                        p = ops[d]
                        if p.kind == "c":
                            if p.eng == "pe" and engname == "pe" and op.kind == "c":
                                continue
                            s, v = ("e", p.eng), p.mile
                        else:
                            s, v = ("d", p.dsem), p.dval
                        if v > waits.get(s, 0):
                            waits[s] = v
                    if op.kind in ("d", "cc") and op.dprev > 0:
                        s = ("d", op.dsem)
                        if op.dprev > waits.get(s, 0):
                            waits[s] = op.dprev
                    for s, v in waits.items():
                        if waited.get(s, 0) >= v:
                            continue
                        waited[s] = v
                        sem = esem[s[1]] if s[0] == "e" else dsem[s[1]]
                        eobj.wait_ge(sem, v)
                    ins = op.fn(eobj)
                    if op.kind == "c":
                        if op.flag:
                            ins.then_inc(esem[op.eng], 1)
                    elif op.kind == "d":
                        ins.then_inc(dsem[op.dsem], 16)
                    else:
                        ins.then_inc(dsem[op.dsem], 1)
                if engname == "sp":
                    for s, v in final.items():
                        eobj.wait_ge(dsem[s], v)

            @block.tensor
            def _(e):
                run("pe", e)

            @block.scalar
            def _(e):
                run("act", e)

            @block.vector
            def _(e):
                run("dve", e)

            @block.gpsimd
            def _(e):
                run("pool", e)

            @block.sync
            def _(e):
                run("sp", e)
        return cnt, qcount


class B:
    def __init__(self, S):
        self.S = S
        self.evq = 0

    def mm(self, out, lhsT, rhs, start=True, stop=True, skip=False):
        if skip:
            self.S.c("pe", lambda e: e.matmul(out, lhsT, rhs, start=start, stop=stop, skip_group_check=True),
                     [lhsT, rhs], [out])
        else:
            self.S.c("pe", lambda e: e.matmul(out, lhsT, rhs, start=start, stop=stop), [lhsT, rhs], [out])

    def tr(self, out, in_, ident):
        self.S.c("pe", lambda e: e.transpose(out, in_, ident), [in_, ident], [out])

    def act(self, out, in_, func, scale=None, bias=None, extra=()):
        kw = {}
        rd = [in_] + list(extra)
        if scale is not None:
            kw["scale"] = scale
            if not isinstance(scale, (int, float)):
                rd.append(scale)
        if bias is not None:
            kw["bias"] = bias
            if not isinstance(bias, (int, float)):
                rd.append(bias)
        self.S.c("act", lambda e: e.activation(out, in_, func, **kw), rd, [out])

    def tt(self, out, a, b, op, eng="dve"):
        self.S.c(eng, lambda e: e.tensor_tensor(out, a, b, op), [a, b], [out])

    def stt(self, out, in0, scalar, in1, op0, op1, eng="dve"):
        rd = [in0, in1]
        if not isinstance(scalar, (int, float)):
            rd.append(scalar)
        self.S.c(eng, lambda e: e.scalar_tensor_tensor(out, in0, scalar, in1, op0, op1), rd, [out])

    def ts(self, out, in0, s1, s2, op0, op1=None, eng="dve"):
        rd = [in0]
        for s in (s1, s2):
            if s is not None and not isinstance(s, (int, float)):
                rd.append(s)
        if op1 is None:
            self.S.c(eng, lambda e: e.tensor_scalar(out, in0, s1, None, op0), rd, [out])
        else:
            self.S.c(eng, lambda e: e.tensor_scalar(out, in0, s1, s2, op0, op1), rd, [out])

    def copy(self, out, in_, eng="dve"):
        if eng == "act":
            self.S.c("act", lambda e: e.copy(out, in_), [in_], [out])
        else:
            self.S.c(eng, lambda e: e.tensor_copy(out, in_), [in_], [out])

    def evac(self, out, in_):
        self.evq += 1
        self.copy(out, in_, eng="act" if self.evq % 2 else "dve")

    def recip(self, out, in_, extra=()):
        self.S.c("dve", lambda e: e.reciprocal(out, in_), [in_] + list(extra), [out])

    def memset(self, ap, val, eng="dve"):
        self.S.c(eng, lambda e: e.memset(ap, val), [], [ap])

    def reduce_sum(self, out, in_):
        self.S.c("dve", lambda e: e.reduce_sum(out, in_, axis=mybir.AxisListType.X), [in_], [out])


def dap(t, offset, dims):
    return bass.AP(t, int(offset), [[int(s), int(c)] for s, c in dims])


def build_program(STAGE=99):
    nc = bass.Bass("TRN2", target_bir_lowering=False)
    S = Sched(nc)
    b = B(S)

    def finish():
        stats = S.emit()
        return nc, stats, len(S.ops)
    EI, EO = "ExternalInput", "ExternalOutput"
    x_d = S.dram("x", [NT, D], F32, EI)
    xh_d = S.dram("xh", [NH, D], F32, EI)
    mem_d = S.dram("mem", [256, D], F32, EI)
    ck_d = S.dram("ck", [PAST, 512], F32, EI)
    cv_d = S.dram("cv", [PAST, 512], F32, EI)
    cmk_d = S.dram("cmk", [2, 256, 512], F32, EI)
    cmv_d = S.dram("cmv", [2, 256, 512], F32, EI)
    spool_d = S.dram("spool", [15, 512], F32, EI)
    w_in_d = S.dram("w_in", [2, D, D], F32, EI)
    w_out_d = S.dram("w_out", [2, D, D], F32, EI)
    w_mem_d = S.dram("w_mem_kv", [2, D, D], F32, EI)
    w_ff1_d = S.dram("w_ff1", [2, D, 4 * D], F32, EI)
    w_ff2_d = S.dram("w_ff2", [2, 4 * D, D], F32, EI)
    w_pool_d = S.dram("w_pool", [4, 128, 128], F32, EI)
    w_kv_d = S.dram("w_kv", [D, D], F32, EI)
    lq_d = S.dram("lq", [1, 256], F32, EI)
    cols_d = S.dram("cols", [128, 69], F32, EI)
    qaug_d = S.dram("qaug", [3, 4, NT], BF16, EI)
    kaug_d = S.dram("kaug", [3, 512], BF16, EI)
    dtab_d = S.dram("dtab", [128, 4 * 4 * 128], F32, EI)
    dtabs_d = S.dram("dtabs", [32, 4 * 32], F32, EI)
    rc16_d = S.dram("rc16", [1, 64], F32, EI)
    ident_d = S.dram("identd", [128, 128], F32, EI)

    y_d = S.dram("y", [NT, D], F32, EO)
    memk_d = S.dram("memk", [2, 256, 512], F32, EO)
    memv_d = S.dram("memv", [2, 256, 512], F32, EO)
    poolo_d = S.dram("poolo", [2, 15, 512], F32, EO)
    ko_d = S.dram("ko", [NT, 512], F32, EO)
    vo_d = S.dram("vo", [NT, 512], F32, EO)

    xk_in = [S.dram("xk_in%d" % i, [512, 1024], BF16) for i in range(2)]
    xk_out = [S.dram("xk_out%d" % i, [2048, 1024], BF16) for i in range(2)]
    xv_in = [S.dram("xv_in%d" % i, [256, 2048], BF16) for i in range(2)]
    xv_out = [S.dram("xv_out%d" % i, [1024, 2048], BF16) for i in range(2)]

    XT = S.sbuf("XT", [128, 8, NT], F32)
    RW = S.reserve(46080)
    WSQ = [S.sbuf("WSQ%d" % i, [128, 8, 1024], BF16) for i in range(2)]
    WM = S.reserve(32768)
    MKpL = [S.sbuf("MKp%d" % i, [128, 4, 256], BF16) for i in range(2)]
    MVpL = [S.sbuf("MVp%d" % i, [128, 2, 512], BF16) for i in range(2)]
    MKs = S.sbuf("MKs", [128, 4, 256], BF16)
    MVs = S.sbuf("MVs", [128, 2, 512], BF16)
    DTAB = S.sbuf("DTAB", [128, 4, 4, 128], BF16)
    DTABS = S.sbuf("DTABS", [32, 4, 32], BF16)
    TF = [S.sbuf("TF%d" % i, [128, 512], F32) for i in range(4)]
    SQ = [S.sbuf("SQ%d" % i, [128, 512], BF16) for i in range(3)]
    IDF = S.sbuf("IDF", [128, 128], F32)
    IDB = S.sbuf("IDB", [128, 128], BF16)
    ONES = S.sbuf("ONES", [128, 128], BF16)
    OND1 = S.sbuf("OND1", [128, 128], BF16)
    OND2 = S.sbuf("OND2", [128, 128], BF16)
    COLS = S.sbuf("COLS", [128, 72], F32)
    LQ = S.sbuf("LQ", [128, 256], F32)
    LT = S.sbuf("LT", [128, 64], F32)
    LS = S.sbuf("LS", [128, 8], F32)
    RC16 = S.sbuf("RC16", [128, 4, 16], F32)
    WPOOL = S.sbuf("WPOOL", [128, 4, 128], BF16)
    KSN0 = S.sbuf("KSN0", [128, 4, 32], BF16)
    KSN1 = S.sbuf("KSN1", [128, 4, 32], BF16)
    VSN = S.sbuf("VSN", [32, 512], BF16)
    SMALL = S.sbuf("SMALL", [128, 4, 16], F32)

    XNT = S.sbuf("XNT", [128, 8, 512], BF16, at=RW + 0)
    TM = S.sbuf("TM", [128, 8, 512], BF16, at=RW + 0)
    U = S.sbuf("U", [128, 4, 4, 144], F32, at=RW + 8192)
    UH = S.sbuf("UH", [128, 4, 16, 16], F32, at=RW + 17408)
    PA = S.sbuf("PA", [128, 4, 4, 144], F32, at=RW + 21504)
    PB = S.sbuf("PB", [128, 3, 4, 144], F32, at=RW + 30720)
    DD = S.sbuf("DD", [128, 4, 512], BF16, at=RW + 37632)
    QM = S.sbuf("QM", [128, 4, 512], BF16, at=RW + 41984)
    TMm1 = S.sbuf("TMm1", [128, 4, 512], BF16, at=RW + 37888)
    TMt1 = S.sbuf("TMt1", [128, 4, 512], BF16, at=WM + 28672)
    XNTB = S.sbuf("XNTB", [128, 8, 512], BF16, at=RW + 20480)
    Q0 = S.sbuf("Q0", [128, 4, 512], BF16, at=RW + 8192)
    Q1 = S.sbuf("Q1", [128, 4, 512], BF16, at=RW + 12288)
    OA = S.sbuf("OA", [128, 512], F32, at=RW + 16384)
    OB = S.sbuf("OB", [128, 512], F32, at=RW + 18432)
    XN = S.sbuf("XN", [128, 8, NT], BF16, at=RW + 0)
    HQ = [S.sbuf("HQ%d" % i, [128, 4, 512], BF16, at=RW + 33280 + 4096 * i) for i in range(2)]
    RL = [S.sbuf("RL%d" % i, [128, 512], F32, at=RW + 41472 + 2048 * i) for i in range(2)]
    XHT = S.sbuf("XHT", [128, 8, 256], F32, at=RW + 21504)
    XNH = S.sbuf("XNH", [128, 8, 256], BF16, at=RW + 29696)
    MEMT = S.sbuf("MEMT", [128, 8, 256], F32, at=RW + 0)
    MEMN = S.sbuf("MEMN", [128, 8, 256], BF16, at=RW + 8192)
    YN = S.sbuf("YN", [128, 8, 512], F32, at=RW + 0)
    WM1 = [S.sbuf("WM1_%d" % i, [128, 8, 512], BF16, at=WM + 8192 * i) for i in range(2)]
    WM2 = [S.sbuf("WM2_%d" % i, [128, 4, 1024], BF16, at=WM + 16384 + 8192 * i) for i in range(2)]
    KB0 = [S.sbuf("KB0_%d" % i, [128, 512], BF16, at=WM + 1024 * i) for i in range(NKBUF)]
    KB1 = [S.sbuf("KB1_%d" % i, [128, 512], BF16, at=WM + 4096 + 1024 * i) for i in range(NKBUF)]
    VB = [S.sbuf("VB_%d" % i, [128, 4, 128], BF16, at=WM + 8192 + 1024 * i) for i in range(NKBUF)]
    EB = [S.sbuf("EB_%d" % i, [128, 512], BF16, at=WM + 12288 + 1024 * i) for i in range(8)]
    CKS = S.sbuf("CKS", [128, 4, 512], F32, at=WM + 20480)
    CVSF = S.sbuf("CVSF", [128, 4, 512], F32, at=RW + 21504)
    NEB = 6
    EBP = [S.sbuf("EBP%d" % i, [128, 2, 512], BF16, at=WM + 12288 + 2048 * i) for i in range(NEB)]
    ESP = [S.sbuf("ESP%d" % i, [128, 2, 512], BF16, at=WM + 24576 + 2048 * i) for i in range(2)]
    ACC = S.sbuf("ACC", [128, 2, 512], F32, at=RW + 29696)
    HI = S.sbuf("HI", [128, 2, 512], BF16, at=RW + 33792)
    LO = S.sbuf("LO", [128, 2, 512], BF16, at=RW + 35840)
    XIN = S.sbuf("XIN", [128, 4, 1024], F32, at=WM + 0)
    XIN2 = S.sbuf("XIN2", [128, 4, 1024], F32, at=WM + 16384)
    STG = [S.sbuf("STG%d" % i, [128, 1024], F32, at=WM + 16384 + 4096 * i) for i in range(2)]
    VST = [S.sbuf("VST%d" % i, [128, 512], BF16, at=WM + 24576 + 1024 * i) for i in range(2)]
    KST = [S.sbuf("KST%d" % i, [128, 512], BF16, at=WM + 26624 + 1024 * i) for i in range(2)]
    MST = S.sbuf("MST", [128, 2, 1024], F32, at=WM + 0)
    CMS = S.sbuf("CMS", [128, 2, 512], F32, at=WM + 8192)

    PP = [S.psum("PP%d" % i, (128, 1024)) for i in range(4)]
    PS = [PP[i // 2][:, (i % 2) * 512:(i % 2) * 512 + 512] for i in range(8)]
    psr = [0]

    def bank():
        psr[0] += 1
        return PS[psr[0] % 8]

    def gcol(i):
        return COLS[:, i:i + 1]

    S.dma("sp", IDF[:], ident_d[:, :])
    S.dma("sp", COLS[:, 0:69], cols_d[:, :])
    S.dma("sp", LQ[:], dap(lq_d, 0, [(0, 128), (1, 256)]))
    S.dma("sp", RC16[:], dap(rc16_d, 0, [(0, 128), (16, 4), (1, 16)]))
    def load_sq(dst, src_t, off):
        for kc in range(8):
            S.dma("pool", dst[:, kc, :], dap(src_t, off + kc * 128 * 1024, [(1024, 128), (1, 1024)]))

    load_sq(WSQ[0], w_in_d, 0)
    load_sq(WSQ[1], w_mem_d, 0)
    S.dma("pool", WPOOL[:], dap(w_pool_d, 0, [(128, 128), (16384, 4), (1, 128)]))
    S.dma("pool", DTAB[:], dap(dtab_d, 0, [(2048, 128), (512, 4), (128, 4), (1, 128)]))
    S.dma("pool", DTABS[:], dap(dtabs_d, 0, [(128, 32), (32, 4), (1, 32)]))
    b.copy(IDB[:], IDF[:])
    b.memset(ONES[:], 1.0)
    b.memset(OND1[:], 1.0 / 1024)
    b.memset(OND2[:], 1.0 / 128)
    b.tt(LT[:], LQ[:, 0:64], LQ[:, 64:128], ALU.mult)
    b.reduce_sum(LS[:, 0:1], LT[:])
    b.tt(LT[:], LQ[:, 128:192], LQ[:, 192:256], ALU.mult)
    b.reduce_sum(LS[:, 1:2], LT[:])
    b.act(LS[:, 4:6], LS[:, 0:2], ACTF.Exp)
    b.tt(LS[:, 6:7], LS[:, 5:6], LS[:, 4:5], ALU.subtract)
    b.ts(LS[:, 2:3], LS[:, 6:7], -LAM_I, None, ALU.add)
    b.ts(LS[:, 3:4], gcol(68), 1.0 - LAM_I, None, ALU.mult)
    NEGLAM = LS[:, 2:3]
    GSUB = LS[:, 3:4]

    sqr = [0]

    def norm(src, dst, n, g0, invn_ones, nchunks=8):
        ps = bank()
        for c in range(nchunks):
            sq = SQ[sqr[0] % 3]
            sqr[0] += 1
            b.act(sq[:, :n], src(c), ACTF.Square)
            b.mm(ps[:, :n], invn_ones[:], sq[:, :n], start=(c == 0), stop=(c == nchunks - 1))
        b.act(TF[0][:, :n], ps[:, :n], ACTF.Ln, bias=EPS)
        b.act(TF[1][:, :n], TF[0][:, :n], ACTF.Exp, scale=-0.5)
        for c in range(nchunks):
            b.stt(dst(c), src(c), gcol(g0 + c), TF[1][:, :n], ALU.mult, ALU.mult)

    for t, (c0, n) in enumerate(TILES):
        if n == 512:
            xin = XIN if t % 2 == 0 else XIN2
            S.dma("sp", xin[:], dap(x_d, c0 * D, [(D, 128), (128 * D, 4), (1, D)]))
            for c in range(8):
                ps = bank()
                for blk in range(4):
                    b.tr(ps[:, blk * 128:(blk + 1) * 128], xin[:, blk, c * 128:(c + 1) * 128], IDF[:])
                b.evac(XT[:, c, c0:c0 + 512], ps[:, :])
        else:
            S.dma("sp", XIN[0:32, 0, :], dap(x_d, c0 * D, [(D, 32), (1, D)]))
            ps = bank()
            for c in range(8):
                b.tr(ps[:, c * 32:(c + 1) * 32], XIN[0:32, 0, c * 128:(c + 1) * 128], IDF[0:32, 0:32])
            b.evac(XT[:, :, c0:c0 + 32], ps[:, 0:256].rearrange("p (c t) -> p c t", c=8))
    S.dma("sp", XIN[:, 0:2, :], dap(xh_d, 0, [(D, 128), (128 * D, 2), (1, D)]))
    for c in range(8):
        ps = bank()
        for blk in range(2):
            b.tr(ps[:, blk * 128:(blk + 1) * 128], XIN[:, blk, c * 128:(c + 1) * 128], IDF[:])
        b.evac(XHT[:, c, :], ps[:, 0:256])

    def mem_kv_prompt(l, first, W):
        if first:
            S.dma("sp", MST[:], dap(mem_d, 0, [(D, 128), (128 * D, 2), (1, D)]))
            for c in range(8):
                ps = bank()
                for mb in range(2):
                    b.tr(ps[:, mb * 128:(mb + 1) * 128], MST[:, mb, c * 128:(c + 1) * 128], IDF[:])
                b.evac(MEMT[:, c, :], ps[:, 0:256])
        norm(lambda c: MEMT[:, c, :], lambda c: MEMN[:, c, :], 256, 16 + 8 * l, OND1)
        MKp, MVp = MKpL[l], MVpL[l]
        for h in range(4):
            ps = bank()
            for kc in range(8):
                b.mm(ps[:, 0:256], W[:, kc, h * 128:(h + 1) * 128], MEMN[:, kc, :], start=(kc == 0), stop=(kc == 7))
            b.evac(MKp[:, h, :], ps[:, 0:256])
        for mb in range(2):
            stg = STG[mb]
            for half in range(2):
                ps = bank()
                for kc in range(8):
                    b.mm(ps[:, :], MEMN[:, kc, mb * 128:(mb + 1) * 128], W[:, kc, half * 512:(half + 1) * 512],
                         start=(kc == 0), stop=(kc == 7))
                b.evac(stg[:, half * 512:(half + 1) * 512], ps[:, :])
            b.copy(MVp[:, mb, :], stg[:, 512:1024], eng="act")
            S.dma("sp", dap(memk_d, l * 256 * 512 + mb * 128 * 512, [(512, 128), (1, 512)]), stg[:, 0:512])
            S.dma("sp", dap(memv_d, l * 256 * 512 + mb * 128 * 512, [(512, 128), (1, 512)]), stg[:, 512:1024])

    def mem_kv_sample(l):
        S.dma("sp", CMS[:], dap(cmk_d, l * 256 * 512, [(512, 128), (128 * 512, 2), (1, 512)]))
        S.dma("pool", MVs[:], dap(cmv_d, l * 256 * 512, [(512, 128), (128 * 512, 2), (1, 512)]))
        for h in range(4):
            ps = bank()
            for mb in range(2):
                b.tr(ps[:, mb * 128:(mb + 1) * 128], CMS[:, mb, h * 128:(h + 1) * 128], IDF[:])
            b.evac(MKs[:, h, :], ps[:, 0:256])

    ebr = [0]

    def mem_attend(n, MK, MV):
        for h in range(4):
            es = []
            for mb in range(2):
                ps = bank()
                b.mm(ps[:, :n], MK[:, h, mb * 128:(mb + 1) * 128], QM[:, h, :n])
                e = SQ[sqr[0] % 3]
                sqr[0] += 1
                b.act(e[:, :n], ps[:, :n], ACTF.Exp, scale=128.0 ** -0.5)
                es.append(e)
            po = bank()
            pd = bank()
            for mb in range(2):
                b.mm(po[:, :n], MV[:, mb, h * 128:(h + 1) * 128], es[mb][:, :n], start=(mb == 0), stop=(mb == 1))
            for mb in range(2):
                b.mm(pd[:, :n], ONES[:], es[mb][:, :n], start=(mb == 0), stop=(mb == 1))
            b.act(TF[2][:, :n], pd[:, :n], ACTF.Ln)
            b.act(TF[2][:, :n], TF[2][:, :n], ACTF.Exp, scale=-1.0)
            b.tt(TMV[4 + h][:, :n], po[:, :n], TF[2][:, :n], ALU.mult)

    def proj(W, ocs, rhs, n, evac_fn):
        for oc in ocs:
            ps = bank()
            for kc in range(8):
                b.mm(ps[:, :n], W[:, kc, oc * 128:(oc + 1) * 128], rhs(kc), start=(kc == 0), stop=(kc == 7))
            evac_fn(oc, ps)

    TMV = [None] * 8

    def wout_residual(W, c0, n, mid=None):
        def ev(oc, ps):
            b.tt(XT[:, oc, c0:c0 + n], ps[:, :n], XT[:, oc, c0:c0 + n], ALU.add)
        proj(W, range(4), lambda kc: TMV[kc][:, :n], n, ev)
        if mid is not None:
            mid()
        proj(W, range(4, 8), lambda kc: TMV[kc][:, :n], n, ev)

    def mlp(l, after_y=None):
        def load_group(hg):
            w1 = WM1[hg % 2]
            w2 = WM2[hg % 2]
            for kc in range(8):
                S.dma("pool", w1[:, kc, :],
                      dap(w_ff1_d, l * D * 4 * D + kc * 128 * 4 * D + hg * 512, [(4 * D, 128), (1, 512)]))
            for hb in range(4):
                S.dma("pool", w2[:, hb, :],
                      dap(w_ff2_d, l * 4 * D * D + (hg * 512 + hb * 128) * D, [(D, 128), (1, D)]))
        MT = [(416 * i, 416) for i in range(5)]
        units = [(hg, t) for hg in range(8) for t in range(len(MT))]
        load_group(0)
        load_group(1)

        def emit_h(u, k):
            hg, t = u
            c0, n = MT[t]
            hq = HQ[k % 2]
            for hb in range(4):
                ps = bank()
                for kc in range(8):
                    b.mm(ps[:, :n], WM1[hg % 2][:, kc, hb * 128:(hb + 1) * 128], XN[:, kc, c0:c0 + n],
                         start=(kc == 0), stop=(kc == 7))
                rl = RL[hb % 2]
                b.act(rl[:, :n], ps[:, :n], ACTF.Relu)
                b.act(hq[:, hb, :n], rl[:, :n], ACTF.Square)

        def emit_y(u, k):
            hg, t = u
            c0, n = MT[t]
            hq = HQ[k % 2]
            for oc in range(8):
                ps = bank()
                for hb in range(4):
                    b.mm(ps[:, :n], WM2[hg % 2][:, hb, oc * 128:(oc + 1) * 128], hq[:, hb, :n],
                         start=(hb == 0), stop=(hb == 3))
                b.tt(XT[:, oc, c0:c0 + n], ps[:, :n], XT[:, oc, c0:c0 + n], ALU.add)

        for k, u in enumerate(units):
            emit_h(u, k)
            if k >= 1:
                emit_y(units[k - 1], k - 1)
                pu = units[k - 1]
                if pu[1] == len(MT) - 1 and pu[0] + 2 < 8:
                    load_group(pu[0] + 2)
                if after_y is not None and pu[0] == 7:
                    after_y(MT[pu[1]])
        emit_y(units[-1], len(units) - 1)
        if after_y is not None:
            after_y(MT[-1])

    mem_kv_prompt(0, True, WSQ[1])
    mem_kv_sample(0)
    load_sq(WSQ[1], w_out_d, 0)

    if STAGE < 1:
        return finish()
    norm(lambda c: XHT[:, c, :], lambda c: XNH[:, c, :], 256, 0, OND1)

    def ev_halo(oc, ps):
        b.evac(UH[:, oc, :, :], ps[:, 0:256].rearrange("p (b t) -> p b t", b=16))
    proj(WSQ[0], range(4), lambda kc: XNH[:, kc, :], 256, ev_halo)


    def pool_tile(t, c0, n):
        nb = 4 if n == 512 else 1
        tw = 144 if n == 512 else 48
        if n == 512:
            b.copy(U[:, :, :, 0:16], UH[:, :, 4 * t:4 * t + 4, :])
        Uv = U[:, :, 0:nb, 0:tw]
        PAv = PA[:, :, 0:nb, 0:tw]
        PBv = PB[:, :, 0:nb, 0:tw]
        b.tt(PAv[:, :, :, 1:tw], Uv[:, :, :, 1:tw], Uv[:, :, :, 0:tw - 1], ALU.add)
        b.tt(PBv[:, :, :, 3:tw], PAv[:, 1:4, :, 3:tw], PAv[:, 1:4, :, 1:tw - 2], ALU.add)
        def dd(g, srcv, w):
            b.stt(DD[:, g, 0:n].rearrange("p (b t) -> p b t", b=nb), srcv, 1.0 / w,
                  U[:, g, 0:nb, 16:tw], ALU.mult, ALU.subtract)
            if t == 0:
                b.tt(SMALL[:, g, :], srcv[:, 0, 0:16], RC16[:, g, :], ALU.mult)
                b.tt(DD[:, g, 0:16], SMALL[:, g, :], U[:, g, 0, 16:32], ALU.subtract)
        dd(0, PA[:, 0, 0:nb, 16:tw], 2)
        dd(1, PB[:, 0, 0:nb, 16:tw], 4)
        b.tt(PAv[:, 2:4, :, 7:tw], PBv[:, 1:3, :, 7:tw], PBv[:, 1:3, :, 3:tw - 4], ALU.add)
        dd(2, PA[:, 2, 0:nb, 16:tw], 8)
        b.tt(PBv[:, 0:1, :, 15:tw], PAv[:, 3:4, :, 15:tw], PAv[:, 3:4, :, 7:tw - 8], ALU.add)
        dd(3, PB[:, 0, 0:nb, 16:tw], 16)

    def pool_mm(n):
        for g in range(4):
            ps = bank()
            b.mm(ps[:, :n], WPOOL[:, g, :], DD[:, g, 0:n])
            b.act(TMV[g][:, :n], ps[:, :n], ACTF.Copy, scale=gcol(64 + g))

    def pool_out(slot, blk, col0):
        ps = bank()
        for g in range(4):
            b.tr(ps[0:15, g * 128:(g + 1) * 128], U[:, g, blk, col0:col0 + 15], IDF[:])
        b.evac(TF[3][0:15, :], ps[0:15, :])
        S.dma("sp", dap(poolo_d, slot * 15 * 512, [(512, 15), (1, 512)]), TF[3][0:15, :])

    XNT0 = [S.sbuf("XNT0_%d" % i, [128, 8, 512], BF16, at=WM + 8192 * i) for i in range(2)]
    TM0 = S.sbuf("TM0", [128, 8, 512], BF16, at=WM + 16384)
    for kc in range(8):
        TMV[kc] = TM0[:, kc, :]
    norm(lambda c: XT[:, c, 0:512], lambda c: XNT0[0][:, c, :512], 512, 0, OND1)
    for t, (c0, n) in enumerate(TILES):
        samp = n != 512
        xn = XNT0[t % 2]
        if samp:
            S.dma("sp", TF[3][0:15, :], dap(spool_d, 0, [(512, 15), (1, 512)]))
            ps = bank()
            for g in range(4):
                b.tr(ps[:, g * 16:g * 16 + 15], TF[3][0:15, g * 128:(g + 1) * 128], IDF[0:15, 0:15])
            b.evac(U[:, :, 0, 1:16], ps[:, 0:64].rearrange("p (g t) -> p g t", g=4)[:, :, 0:15])

        def ev0(oc, ps, n=n, samp=samp):
            if oc < 4:
                if not samp:
                    b.evac(U[:, oc, :, 16:144], ps[:, :].rearrange("p (b t) -> p b t", b=4))
                else:
                    b.evac(U[:, oc, 0, 16:48], ps[:, 0:32])
            else:
                b.evac(QM[:, oc - 4, :n], ps[:, :n])
        proj(WSQ[0], range(8), lambda kc: xn[:, kc, :n], n, ev0)
        pool_tile(t, c0, n)
        mem_attend(n, MKs if samp else MKpL[0], MVs if samp else MVpL[0])
        pool_mm(n)
        mid0 = None
        if t + 1 < len(TILES):
            def mid0(t=t):
                c1, n1 = TILES[t + 1]
                xn1 = XNT0[(t + 1) % 2]
                norm(lambda c: XT[:, c, c1:c1 + n1], lambda c: xn1[:, c, :n1], n1, 0, OND1)
        if t == 3:
            pool_out(0, 3, 129)
        if samp:
            pool_out(1, 0, 33)
        wout_residual(WSQ[1], c0, n, mid=mid0)
    for kc in range(4):
        TMV[kc] = TMt1[:, kc, :]
        TMV[4 + kc] = TMm1[:, kc, :]

    if STAGE < 2:
        return finish()
    load_sq(WSQ[0], w_mem_d, D * D)
    mem_kv_prompt(1, True, WSQ[0])
    for t, (c0, n) in enumerate(TILES):
        norm(lambda c: XT[:, c, c0:c0 + n], lambda c: XN[:, c, c0:c0 + n], n, 32, OND1)
    load_sq(WSQ[0], w_kv_d, 0)
    load_sq(WSQ[1], w_in_d, D * D)
    def kv_norm(tile):
        c0, n = tile
        norm(lambda c: XT[:, c, c0:c0 + n], lambda c: XN[:, c, c0:c0 + n], n, 48, OND1)
    mlp(0, after_y=kv_norm)

    if STAGE < 3:
        return finish()
    mem_kv_sample(1)
    kq = [0]

    def exchange(pairs):
        for (ti, to) in pairs:
            S.collective("pool", lambda e, ti=ti, to=to: e.collective_compute(
                "AllGather", ALU.bypass, replica_groups=[[0, 1, 2, 3], [4, 5, 6, 7]],
                ins=[ti.ap().opt()], outs=[to.ap().opt()]), [ti.ap()], [to.ap()])

    def kv_kt(t):
        c0, n = TILES[t]
        samp = n != 512

        def evk(oc, ps):
            if samp:
                b.copy(KSN0[0:64, oc, :], ps[0:64, 0:32], eng="act")
                b.copy(KSN1[64:128, oc, :], ps[64:128, 0:32], eng="dve")
            else:
                ks = KST[kq[0] % 2]
                kq[0] += 1
                b.evac(ks[:, :], ps[:, :])
                S.dma("sp", dap(xk_in[t // 2], oc * 128 * 1024 + (c0 % 1024), [(1024, 128), (1, 512)]), ks[:, :])
        proj(WSQ[0], range(4), lambda kc: XN[:, kc, c0:c0 + n], n, evk)

    def kv_tok(tb):
        r0 = tb * 128
        rows = 128 if tb < 16 else 32
        stg = STG[tb % 2]
        for half in range(2):
            ps = bank()
            for kc in range(8):
                b.mm(ps[0:rows, :], XN[:, kc, r0:r0 + rows], WSQ[0][:, kc, half * 512:(half + 1) * 512],
                     start=(kc == 0), stop=(kc == 7))
            b.evac(stg[0:rows, half * 512:(half + 1) * 512], ps[0:rows, :])
        S.dma("sp", dap(ko_d, r0 * 512, [(512, rows), (1, 512)]), stg[0:rows, 0:512])
        S.dma("sp", dap(vo_d, r0 * 512, [(512, rows), (1, 512)]), stg[0:rows, 512:1024])
        if tb < 16:
            vs = VST[tb % 2]
            b.copy(vs[:, :], stg[:, 512:1024], eng="act")
            S.dma("sp", dap(xv_in[tb // 8], (r0 % 1024) * 512, [(512, 128), (1, 512)]), vs[:, :])
        else:
            b.copy(VSN[:, :], stg[0:32, 512:1024], eng="act")

    kv_kt(0)
    kv_kt(1)
    for tb in range(8):
        kv_tok(tb)
    exchange([(xk_in[0], xk_out[0]), (xv_in[0], xv_out[0])])
    kv_kt(2)
    kv_kt(3)
    kv_kt(4)
    for tb in range(8, 17):
        kv_tok(tb)
    load_sq(WSQ[0], w_out_d, D * D)
    exchange([(xk_in[1], xk_out[1]), (xv_in[1], xv_out[1])])

    if STAGE < 4:
        return finish()

    for i in range(NKBUF):
        b.memset(KB1[i][0:64, :], 0.0)
        S.dma("sp", KB0[i][64:67, :], dap(kaug_d, 0, [(512, 3), (1, 512)]))
        S.dma("sp", KB1[i][0:3, :], dap(kaug_d, 0, [(512, 3), (1, 512)]))
    b.memset(KSN1[0:64, :, :], 0.0)
    for h in range(4):
        S.dma("sp", KSN0[64:67, h, :], dap(kaug_d, 0, [(512, 3), (1, 32)]))
        S.dma("sp", KSN1[0:3, h, :], dap(kaug_d, 0, [(512, 3), (1, 32)]))
    b.memset(Q1[0:64, :, :], 0.0)

    kbr = [0]

    def finish_head(h, n, po0, po1, pd0, pd1, extra=(), lnbank=None):
        b.act(TF[2][:, :n], pd0, ACTF.Ln, extra=extra)
        b.act(TF[2][:, :n], TF[2][:, :n], ACTF.Exp, scale=-1.0)
        b.tt(OA[:, :n], po0, TF[2][:, :n], ALU.mult)
        b.act(TF[3][:, :n], pd1, ACTF.Ln)
        b.act(TF[3][:, :n], TF[3][:, :n], ACTF.Exp, scale=-1.0)
        b.tt(OB[:, :n], po1, TF[3][:, :n], ALU.mult)
        b.stt(OA[:, :n], OB[:, :n], NEGLAM, OA[:, :n], ALU.mult, ALU.add)
        sq = SQ[sqr[0] % 3]
        sqr[0] += 1
        b.act(sq[:, :n], OA[:, :n], ACTF.Square)
        ps = lnbank if lnbank is not None else bank()
        b.mm(ps[:, :n], OND2[:], sq[:, :n])
        b.act(TF[0][:, :n], ps[:, :n], ACTF.Ln, bias=EPS)
        b.act(TF[1][:, :n], TF[0][:, :n], ACTF.Exp, scale=-0.5)
        b.stt(TMV[h][:, :n], OA[:, :n], GSUB, TF[1][:, :n], ALU.mult, ALU.mult)

    def diff_attend_prompt(m):
        n = 512
        pending = [None]
        for h in range(4):
            items = []
            past = [(r, g) for g in range(m) for r in range(4)]
            diag = [(r, m) for r in range(4)]
            units = []
            kq_ = len(past) // 4
            for q in range(4):
                units += past[q * kq_:(q + 1) * kq_] + [diag[q]]
            SP = [PP[0], PP[1], PP[2]]
            PO0, PO1 = PS[6], PS[7]
            loaded = {}

            def load_unit(ui, h=h):
                r, g = units[ui]
                k = kbr[0] % NKBUF
                kbr[0] += 1
                kt = xk_out[g // 2]
                krow = r * 512 + h * 128
                S.dma("sp", KB0[k][0:64, :], dap(kt, krow * 1024 + 512 * (g % 2), [(1024, 64), (1, 512)]))
                S.dma("sp", KB1[k][64:128, :], dap(kt, (krow + 64) * 1024 + 512 * (g % 2), [(1024, 64), (1, 512)]))
                S.dma("sp", VB[k][:, :, :], dap(xv_out[g // 2], r * 256 * 2048 + 512 * (g % 2) * 512 + h * 128,
                                                 [(512, 128), (128 * 512, 4), (1, 128)]))
                loaded[ui] = k
            for ui, (r, g) in enumerate(units):
                for u in range(4):
                    kb = 4 * (4 * g + u) + r
                    if g < m:
                        items.append((ui, u, kb, 0, None))
                    else:
                        items.append((ui, u, kb, u * 128, r))
            nit = len(items)
            for ui in range(min(NKBUF - 1, len(units))):
                load_unit(ui)
            nxt = min(NKBUF - 1, len(units))
            hist = {}
            e0 = None
            for i in range(nit + 2):
                cur = None
                if i < nit:
                    ui, u, kb, lo, r = items[i]
                    k = loaded[ui]
                    bias = SLOPES[h] * 128.0 * kb
                    sp = SP[i % 3]
                    for c in range(2):
                        off = c * 512
                        KBc = KB0[k][0:67, u * 128:(u + 1) * 128] if c == 0 else KB1[k][:, u * 128:(u + 1) * 128]
                        Qc = Q0[0:67, h, lo:n] if c == 0 else Q1[:, h, lo:n]
                        b.mm(sp[:, off + lo:off + n], KBc, Qc, start=True, stop=(r is None))
                        if r is not None:
                            b.mm(sp[:, off + lo:off + lo + 128], IDB[:], DTAB[:, h, r, :], start=False, stop=True)
                    e = EBP[ebr[0] % NEB]
                    ebr[0] += 1
                    spv = sp[:, :].rearrange("p (c t) -> p c t", c=2)
                    b.act(e[:, :, lo:n], spv[:, :, lo:n], ACTF.Exp, bias=bias)
                    esb = ESP[ui % 2]
                    if u == 0:
                        e0 = e
                    elif u == 1:
                        if lo > 0:
                            b.copy(esb[:, :, 0:lo], e0[:, :, 0:lo], eng="dve")
                        b.tt(esb[:, :, lo:n], e0[:, :, lo:n], e[:, :, lo:n], ALU.add, eng="pool")
                    else:
                        b.tt(esb[:, :, lo:n], esb[:, :, lo:n], e[:, :, lo:n], ALU.add,
                             eng="pool" if u == 2 else "dve")
                        if u == 3:
                            if ui == 0:
                                b.copy(ACC[:], esb[:])
                            else:
                                b.tt(ACC[:], ACC[:], esb[:], ALU.add)
                    hist[i] = (k, u, lo, e, ui)
                if i - 2 >= 0:
                    pk, pu, plo, pe_, pui = hist.pop(i - 2)
                    first = (i - 2 == 0)
                    last = (i - 2 == nit - 1)
                    b.mm(PO0[:, plo:n], VB[pk][:, pu, :], pe_[:, 0, plo:n], start=first, stop=last)
                    b.mm(PO1[:, plo:n], VB[pk][:, pu, :], pe_[:, 1, plo:n], start=first, stop=last)
                if i < nit and items[i][1] == 2 and nxt < len(units):
                    load_unit(nxt)
                    nxt += 1
                if i == 6 and pending[0] is not None:
                    pending[0]()
                    pending[0] = None
            b.copy(OA[:, :n], PO0[:, :n], eng="act")
            b.copy(OB[:, :n], PO1[:, :n], eng="dve")

            b.copy(HI[:], ACC[:])
            b.tt(LO[:], ACC[:], HI[:], ALU.subtract)

            def epilogue(h=h):
                pd = [PS[4], PS[5]]
                for c in range(2):
                    b.mm(pd[c][:, :n], ONES[:], HI[:, c, :], start=True, stop=False)
                    b.mm(pd[c][:, :n], ONES[:], LO[:, c, :], start=False, stop=True)
                finish_head(h, n, OA[:, :n], OB[:, :n], pd[0][:, :n], pd[1][:, :n], lnbank=PS[4])
            if h < 3:
                pending[0] = epilogue
            else:
                epilogue()

    def diff_attend_sample():
        n = 32
        POD = PS[4]
        PT = [PS[0], PS[1]]
        PSS = [PS[2], PS[3]]
        EBS = [S.sbuf("EBS%d" % i, [128, 256], BF16, at=WM + 12288 + 512 * i) for i in range(4)]
        units = [(g, h) for g in range(4) for h in range(4)]
        kmap = {}

        def prep(ni):
            g, h = units[ni]
            if h == 0:
                S.dma("sp", CKS[:], dap(ck_d, g * 512 * 512, [(512, 128), (128 * 512, 4), (1, 512)]))
                S.dma("sp", CVSF[:], dap(cv_d, g * 512 * 512, [(512, 128), (128 * 512, 4), (1, 512)]))
            k = kbr[0] % NKBUF
            kbr[0] += 1
            kmap[ni] = k
            b.copy(VB[k][:, :, :], CVSF[:, :, h * 128:(h + 1) * 128], eng="dve")
            ps = PT[ni % 2]
            for u in range(4):
                b.tr(ps[:, u * 128:(u + 1) * 128], CKS[:, u, h * 128:(h + 1) * 128], IDF[:])
            b.copy(KB0[k][0:64, :], ps[0:64, :], eng="act")
            b.copy(KB1[k][64:128, :], ps[64:128, :], eng="dve")

        def scores(ni):
            g, h = units[ni]
            k = kmap[ni]
            ps = PSS[ni % 2]
            e = EBS[ni % 4]
            for u in range(4):
                for c in range(2):
                    o = (u * 2 + c) * 32
                    b.mm(ps[:, o:o + n], KB0[k][0:67, u * 128:(u + 1) * 128] if c == 0 else KB1[k][:, u * 128:(u + 1) * 128],
                         Q0[0:67, h, 0:n] if c == 0 else Q1[:, h, 0:n])
            for u in range(4):
                b.act(e[:, u * 64:(u + 1) * 64], ps[:, u * 64:(u + 1) * 64], ACTF.Exp,
                      bias=SLOPES[h] * 128.0 * (4 * g + u), extra=[ps[:, 0:256]])

        def av(ni):
            g, h = units[ni]
            k = kmap[ni]
            e = EBS[ni % 4]
            for u in range(4):
                for c in range(2):
                    o = (u * 2 + c) * 32
                    o0 = (h * 2 + c) * 32
                    b.mm(POD[:, o0:o0 + n], VB[k][:, u, :], e[:, o:o + n], start=False, stop=False, skip=True)
                    b.mm(POD[:, 256 + o0:256 + o0 + n], ONES[:], e[:, o:o + n], start=False, stop=False, skip=True)

        b.memset(POD[:, :], 0.0)
        prep(0)
        prep(1)
        for ni in range(len(units)):
            scores(ni)
            if ni + 2 < len(units):
                prep(ni + 2)
            av(ni)
        for h in range(4):
            ps = PSS[h % 2]
            e = EBS[h % 4]
            for c in range(2):
                o = c * 32
                b.mm(ps[0:32, o:o + n], KSN0[0:67, h, :] if c == 0 else KSN1[:, h, :],
                     Q0[0:67, h, 0:n] if c == 0 else Q1[:, h, 0:n], start=True, stop=False)
                b.mm(ps[0:32, o:o + n], IDB[0:32, 0:32], DTABS[:, h, :], start=False, stop=True)
            b.act(e[0:32, 0:64], ps[0:32, 0:64], ACTF.Exp, bias=SLOPES[h] * 128.0 * 16)
            for c in range(2):
                o = c * 32
                o0 = (h * 2 + c) * 32
                b.mm(POD[:, o0:o0 + n], VSN[0:32, h * 128:(h + 1) * 128], e[0:32, o:o + n], start=False, stop=True, skip=True)
                b.mm(POD[:, 256 + o0:256 + o0 + n], ONES[0:32, :], e[0:32, o:o + n], start=False, stop=True, skip=True)
        for h in range(4):
            o0 = (h * 2) * 32
            finish_head(h, n, POD[:, o0:o0 + n], POD[:, o0 + 32:o0 + 32 + n],
                        POD[:, 256 + o0:256 + o0 + n], POD[:, 256 + o0 + 32:256 + o0 + 32 + n], extra=[POD[:, :]])

    b3 = [0]

    def bank3():
        b3[0] += 1
        return PS[b3[0] % 4]

    order = [0, 1, 2, 3, 4]
    XNTL = [XNT, XNTB]
    c00, n00 = TILES[order[0]]
    norm(lambda c: XT[:, c, c00:c00 + n00], lambda c: XNTL[0][:, c, :n00], n00, 8, OND1)
    for idx, t in enumerate(order):
        c0, n = TILES[t]
        samp = n != 512
        xn = XNTL[idx % 2]
        S.dma("sp", Q0[64:67, :, 0:n], dap(qaug_d, c0, [(4 * NT, 3), (NT, 4), (1, n)]))
        S.dma("sp", Q1[0:3, :, 0:n], dap(qaug_d, c0, [(4 * NT, 3), (NT, 4), (1, n)]))

        def ev1(oc, ps, n=n):
            if oc < 4:
                b.act(Q0[0:64, oc, :n], ps[0:64, :n], ACTF.Copy, scale=0.125)
                b.act(Q1[64:128, oc, :n], ps[64:128, :n], ACTF.Copy, scale=0.125)
            else:
                b.evac(QM[:, oc - 4, :n], ps[:, :n])
        proj(WSQ[1], range(8), lambda kc: xn[:, kc, :n], n, ev1)
        mem_attend(n, MKs if samp else MKpL[1], MVs if samp else MVpL[1])
        if samp:
            diff_attend_sample()
        else:
            diff_attend_prompt(t)
        mid1 = None
        if idx + 1 < len(order):
            def mid1(idx=idx):
                c1, n1 = TILES[order[idx + 1]]
                xn1 = XNTL[(idx + 1) % 2]
                norm(lambda c: XT[:, c, c1:c1 + n1], lambda c: xn1[:, c, :n1], n1, 8, OND1)
        wout_residual(WSQ[0], c0, n, mid=mid1)

    if STAGE < 5:
        return finish()
    for t, (c0, n) in enumerate(TILES):
        norm(lambda c: XT[:, c, c0:c0 + n], lambda c: XN[:, c, c0:c0 + n], n, 40, OND1)
    mlp(1)
    if STAGE < 6:
        return finish()

    so = [0]
    YNL = [YN, S.sbuf("YNB", [128, 8, 512], F32, at=RW + 16384)]
    c00, n00 = TILES[0]
    norm(lambda c: XT[:, c, c00:c00 + n00], lambda c: YNL[0][:, c, :n00], n00, 56, OND1)
    for t, (c0, n) in enumerate(TILES):
        YN = YNL[t % 2]
        if t + 1 < len(TILES):
            c1, n1 = TILES[t + 1]
            yn1 = YNL[(t + 1) % 2]
            norm(lambda c: XT[:, c, c1:c1 + n1], lambda c: yn1[:, c, :n1], n1, 56, OND1)
        nb = 4 if n == 512 else 1
        for blk in range(nb):
            rows = 128 if n == 512 else 32
            stg = STG[so[0] % 2]
            so[0] += 1
            for half in range(2):
                ps = bank()
                for cc in range(4):
                    c = half * 4 + cc
                    b.tr(ps[0:rows, cc * 128:(cc + 1) * 128], YN[:, c, blk * 128:blk * 128 + rows], IDF[:])
                b.evac(stg[0:rows, half * 512:(half + 1) * 512], ps[0:rows, :])
            S.dma("sp", dap(y_d, (c0 + blk * 128) * D, [(D, rows), (1, D)]), stg[0:rows, :])

    return finish()


_PROG = {}
_STAGE = [99]


def _host_tables(j):
    pos = np.zeros(NT, np.int64)
    for i in range(16):
        gb = 4 * i + j
        pos[i * 128:(i + 1) * 128] = gb * 128 + np.arange(128)
    pos[NP:] = PAST + np.arange(NSM)
    qa = pos // 128
    qb = pos % 128
    qaug = np.zeros((3, 4, NT), np.float32)
    for h in range(4):
        qaug[0, h] = -SLOPES[h] * 128.0 * qa
        qaug[1, h] = -SLOPES[h] * qb
        qaug[2, h] = SLOPES[h]
    kaug = np.zeros((3, 512), np.float32)
    kaug[0] = 1.0
    kaug[1] = 1.0
    kaug[2] = np.arange(512) % 128
    ii = np.arange(128)[:, None]
    qq = np.arange(128)[None, :]
    dtab = np.zeros((128, 4, 4, 128), np.float32)
    for h in range(4):
        dd = np.where(ii // 64 > qq // 64, NEG, np.where(ii > qq, -2.0 * SLOPES[h] * (ii - qq), 0.0))
        for r in range(4):
            if r < j:
                dtab[:, h, r, :] = 0.0
            elif r == j:
                dtab[:, h, r, :] = dd
            else:
                dtab[:, h, r, :] = NEG
    i2 = np.arange(32)[:, None]
    q2 = np.arange(32)[None, :]
    dtabs = np.zeros((32, 4, 32), np.float32)
    for h in range(4):
        dtabs[:, h, :] = np.where(i2 > q2, -2.0 * SLOPES[h] * (i2 - q2), 0.0)
    rc16 = np.zeros((4, 16), np.float32)
    for g, w in enumerate((2, 4, 8, 16)):
        if j == 0:
            rc16[g] = 1.0 / np.minimum(np.arange(16) + 1, w)
        else:
            rc16[g] = 1.0 / w
    import ml_dtypes
    qaug = qaug.astype(ml_dtypes.bfloat16)
    kaug = kaug.astype(ml_dtypes.bfloat16)
    return dict(qaug=qaug, kaug=kaug, dtab=dtab.reshape(128, -1), dtabs=dtabs.reshape(32, -1),
                rc16=rc16.reshape(1, 64), identd=np.eye(128, dtype=np.float32))


def kernel(x_prompt, x_sample, mem_prompt, cache_k, cache_v, cache_mem_k, cache_mem_v, state_pool,
           g_attn, w_in, w_out, g_mem, w_mem_kv, g_ffn, w_ff1, w_ff2, w_pool, pool_scale,
           lambda_qk, g_subln, g_kv, w_kv, g_final):
    f = lambda a: np.ascontiguousarray(np.asarray(a, dtype=np.float32))
    x_prompt, x_sample, mem_prompt = f(x_prompt), f(x_sample), f(mem_prompt)
    cache_k, cache_v, cache_mem_k, cache_mem_v = f(cache_k), f(cache_v), f(cache_mem_k), f(cache_mem_v)
    state_pool = f(state_pool)
    if "p" not in _PROG:
        _PROG["p"] = build_program(_STAGE[0])
    nc = _PROG["p"][0]

    def colv(v):
        return f(v).reshape(8, 128).T
    cols = np.concatenate([colv(g_attn[0]), colv(g_attn[1]), colv(g_mem[0]), colv(g_mem[1]),
                           colv(g_ffn[0]), colv(g_ffn[1]), colv(g_kv), colv(g_final),
                           f(pool_scale).reshape(4, 128).T, f(g_subln).reshape(1, 128).T], axis=1)
    shared = dict(w_in=f(w_in), w_out=f(w_out), w_mem_kv=f(w_mem_kv), w_ff1=f(w_ff1), w_ff2=f(w_ff2),
                  w_pool=f(w_pool).reshape(4, 128, 128), w_kv=f(w_kv), lq=f(lambda_qk).reshape(1, 256),
                  cols=np.ascontiguousarray(cols.astype(np.float32)))
    in_maps = []
    for c in range(8):
        bq, j = c // 4, c % 4
        xp = x_prompt[bq].reshape(64, 128, D)
        blocks = xp[j::4]
        xx = np.concatenate([blocks.reshape(NP, D), x_sample[c]], axis=0)
        xh = np.zeros((16, 16, D), np.float32)
        for i in range(16):
            gb = 4 * i + j
            if gb > 0:
                xh[i] = x_prompt[bq, gb * 128 - 16:gb * 128]
        m = dict(shared)
        m.update(x=np.ascontiguousarray(xx), xh=xh.reshape(NH, D), mem=mem_prompt[bq],
                 ck=cache_k[c].reshape(PAST, 512), cv=cache_v[c].reshape(PAST, 512),
                 cmk=np.ascontiguousarray(cache_mem_k[:, c].reshape(2, 256, 512)),
                 cmv=np.ascontiguousarray(cache_mem_v[:, c].reshape(2, 256, 512)),
                 spool=state_pool[0, c])
        m.update(_host_tables(j))
        in_maps.append(m)
    res = run_bass_kernel_spmd(nc, in_maps, core_ids=list(range(8)))
    R = res.results
    y_prompt = np.zeros((2, SEQ, D), np.float32)
    k_prompt = np.zeros((2, SEQ, 4, 128), np.float32)
    v_prompt = np.zeros((2, SEQ, 4, 128), np.float32)
    y_sample = np.zeros((8, NSM, D), np.float32)
    k_sample = np.zeros((8, NSM, 4, 128), np.float32)
    v_sample = np.zeros((8, NSM, 4, 128), np.float32)
    pool_sample = np.zeros((1, 8, 15, 512), np.float32)
    for c in range(8):
        bq, j = c // 4, c % 4
        r = R[c]
        yv = y_prompt[bq].reshape(64, 128, D)
        kv = k_prompt[bq].reshape(64, 128, 512)
        vv = v_prompt[bq].reshape(64, 128, 512)
        yv[j::4] = r["y"][:NP].reshape(16, 128, D)
        kv[j::4] = r["ko"][:NP].reshape(16, 128, 512)
        vv[j::4] = r["vo"][:NP].reshape(16, 128, 512)
        y_sample[c] = r["y"][NP:]
        k_sample[c] = r["ko"][NP:].reshape(NSM, 4, 128)
        v_sample[c] = r["vo"][NP:].reshape(NSM, 4, 128)
        pool_sample[0, c] = r["poolo"][1]
    mem_k_prompt = np.stack([R[0]["memk"], R[4]["memk"]], axis=1).reshape(2, 2, 256, 4, 128)
    mem_v_prompt = np.stack([R[0]["memv"], R[4]["memv"]], axis=1).reshape(2, 2, 256, 4, 128)
    pool_prompt = np.stack([R[3]["poolo"][0], R[7]["poolo"][0]], axis=0)[None]
    return (y_prompt, y_sample, np.ascontiguousarray(mem_k_prompt), np.ascontiguousarray(mem_v_prompt),
            np.ascontiguousarray(pool_prompt), k_prompt, v_prompt, pool_sample, k_sample, v_sample)
```

```python
import math
import contextlib
import numpy as np
import concourse.bass as bass
import concourse.mybir as mybir
from concourse.bass_utils import run_bass_kernel_spmd

DT = mybir.dt
F32 = DT.float32
BF16 = DT.bfloat16
ALU = mybir.AluOpType
ACTF = mybir.ActivationFunctionType
ENGS = ("pe", "act", "dve", "pool", "sp")

D = 1024
NP = 2048
NSM = 32
NT = NP + NSM
NH = 256
SEQ = 8192
PAST = 2048
EPS = 1e-6
LAM_I = 0.8 - 0.6 * math.exp(-0.3 * 1)
SLOPES = [2.0 ** (-8.0 * (h + 1) / 4) for h in range(4)]
NEG = -30000.0
TILES = [(0, 512), (512, 512), (1024, 512), (1536, 512), (2048, 32)]
NKBUF = 4


def _prod(xs):
    r = 1
    for x in xs:
        r *= int(x)
    return r


class Op:
    __slots__ = ("idx", "eng", "fn", "kind", "deps", "flag", "mile", "dsem", "dval", "dprev")

    def __init__(self, idx, eng, fn, kind):
        self.idx = idx
        self.eng = eng
        self.fn = fn
        self.kind = kind
        self.deps = set()
        self.flag = False
        self.mile = 0
        self.dsem = None
        self.dval = 0
        self.dprev = 0


class Sched:
    def __init__(self, nc):
        self.nc = nc
        self.ops = []
        self.sb_top = 16512
        self.sb_limit = 229344
        self.tinfo = {}
        self.wr = {}
        self.rd = {}
        self.n_dma_sems = {"sp": 48, "pool": 32, "act": 8}

    def sbuf(self, name, shape, dtype, at=None):
        dsize = DT.size(dtype)
        per_part = _prod(shape[1:]) * dsize
        if at is None:
            at = (self.sb_top + 63) // 64 * 64
            self.sb_top = at + per_part
        assert at + per_part <= self.sb_limit, (name, at, per_part)
        t = self.nc.alloc_sbuf_tensor_at(name, list(shape), dtype, offset=int(at))
        self.tinfo[t.name] = ("SB", int(at), _prod(shape[1:]), dsize)
        return t

    def reserve(self, nbytes):
        at = (self.sb_top + 63) // 64 * 64
        self.sb_top = at + nbytes
        assert self.sb_top <= self.sb_limit, ("reserve", at, nbytes)
        return at

    def psum(self, name, shape=(128, 512), dtype=F32):
        t = self.nc.alloc_psum_tensor(name, list(shape), dtype)
        self.tinfo[t.name] = ("PS:" + t.name, 0, _prod(shape[1:]), DT.size(dtype))
        return t

    def dram(self, name, shape, dtype, kind="Internal"):
        t = self.nc.dram_tensor(name, list(shape), dtype, kind=kind)
        self.tinfo[t.name] = ("DR:" + t.name, 0, None, DT.size(dtype))
        return t

    def regions(self, ap):
        t = ap.tensor
        key, base, per_part, dsize = self.tinfo[t.name]
        off = int(ap.offset)
        dims = [(int(s), int(c)) for s, c in ap.ap]
        if per_part is None:
            lo = hi = off
            for s, c in dims:
                if c > 1:
                    if s >= 0:
                        hi += s * (c - 1)
                    else:
                        lo += s * (c - 1)
            return [(key, 0, 1, lo, hi + 1)]
        p0 = off // per_part
        inoff = off % per_part
        pstride, pcount = dims[0]
        if pstride == 0 or pcount == 1:
            p1 = p0 + 1
        else:
            assert pstride == per_part, (t.name, dims, per_part)
            p1 = p0 + pcount
        lo = hi = inoff
        for s, c in dims[1:]:
            if c > 1:
                if s >= 0:
                    hi += s * (c - 1)
                else:
                    lo += s * (c - 1)
        if key.startswith("PS:"):
            b0 = (lo * dsize) // 2048
            b1 = (hi * dsize) // 2048
            return [(key + ":%d" % bk, p0 // 32 * 32, (p1 + 31) // 32 * 32, 0, 2048) for bk in range(b0, b1 + 1)]
        return [(key, p0, p1, base + lo * dsize, base + (hi + 1) * dsize)]

    def _add(self, eng, fn, reads, writes, kind="c"):
        op = Op(len(self.ops), eng, fn, kind)
        rregs = [rg for a in reads if a is not None for rg in self.regions(a)]
        wregs = [rg for a in writes if a is not None for rg in self.regions(a)]
        deps = op.deps
        for (key, p0, p1, lo, hi) in rregs:
            for w in self.wr.get(key, ()):
                if w[0] < p1 and p0 < w[1] and w[2] < hi and lo < w[3]:
                    deps.add(w[4])
        for (key, p0, p1, lo, hi) in wregs:
            for w in self.wr.get(key, ()):
                if w[0] < p1 and p0 < w[1] and w[2] < hi and lo < w[3]:
                    deps.add(w[4])
            for r in self.rd.get(key, ()):
                if r[0] < p1 and p0 < r[1] and r[2] < hi and lo < r[3]:
                    deps.add(r[4])
        for (key, p0, p1, lo, hi) in wregs:
            wl = self.wr.setdefault(key, [])
            wl[:] = [w for w in wl if not (p0 <= w[0] and w[1] <= p1 and lo <= w[2] and w[3] <= hi)]
            wl.append([p0, p1, lo, hi, op.idx])
            rl = self.rd.get(key)
            if rl:
                rl[:] = [r for r in rl if not (p0 <= r[0] and r[1] <= p1 and lo <= r[2] and r[3] <= hi)]
        inorder = kind == "c"
        for (key, p0, p1, lo, hi) in rregs:
            rl = self.rd.setdefault(key, [])
            done = False
            if inorder:
                for r in rl:
                    if r[5] == eng and r[0] == p0 and r[1] == p1 and r[2] == lo and r[3] == hi:
                        r[4] = op.idx
                        done = True
                        break
            if not done:
                rl.append([p0, p1, lo, hi, op.idx, eng if inorder else None])
        deps.discard(op.idx)
        self.ops.append(op)
        return op

    def c(self, eng, fn, reads, writes):
        return self._add(eng, fn, reads, writes, "c")

    def dma(self, q, out, in_, **kw):
        def fn(e, out=out, in_=in_, kw=kw):
            return e.dma_start(out=out, in_=in_, **kw)
        return self._add(q, fn, [in_], [out], "d")

    def collective(self, q, fn, reads, writes):
        return self._add(q, fn, reads, writes, "cc")

    def emit(self):
        nc = self.nc
        ops = self.ops
        for op in ops:
            for d in op.deps:
                p = ops[d]
                if p.kind == "c":
                    if p.eng == "pe" and op.eng == "pe" and op.kind == "c":
                        continue
                    p.flag = True
        cnt = {e: 0 for e in ENGS}
        for op in ops:
            if op.kind == "c" and op.flag:
                cnt[op.eng] += 1
                op.mile = cnt[op.eng]
        qcount = {}
        ncc = 0
        for op in ops:
            if op.kind == "d":
                k = qcount.get(op.eng, 0)
                qcount[op.eng] = k + 1
                ns = self.n_dma_sems[op.eng]
                op.dsem = (op.eng, k % ns)
                op.dval = 16 * (k // ns + 1)
                op.dprev = 16 * (k // ns)
            elif op.kind == "cc":
                op.dsem = ("cc", ncc)
                ncc += 1
                op.dval = 1
                op.dprev = 0
        with contextlib.ExitStack() as st:
            esem = {e: st.enter_context(nc.semaphore("s_" + e)) for e in ENGS}
            dsem = {}
            for q, n in self.n_dma_sems.items():
                for i in range(min(n, qcount.get(q, 0))):
                    dsem[(q, i)] = st.enter_context(nc.semaphore("d_%s_%d" % (q, i)))
            for i in range(ncc):
                dsem[("cc", i)] = st.enter_context(nc.semaphore("s_cc%d" % i))
            block = st.enter_context(nc.Block())
            per_eng = {e: [op for op in ops if op.eng == e] for e in ENGS}
            final = {}
            for op in ops:
                if op.kind in ("d", "cc"):
                    final[op.dsem] = max(final.get(op.dsem, 0), op.dval)

            def run(engname, eobj):
                waited = {}
                for op in per_eng[engname]:
                    waits = {}
                    for d in op.deps:
                        p = ops[d]
                        if p.kind == "c":
                            if p.eng == "pe" and engname == "pe" and op.kind == "c":
                                continue
                            s, v = ("e", p.eng), p.mile
                        else:
                            s, v = ("d", p.dsem), p.dval
                        if v > waits.get(s, 0):
                            waits[s] = v
                    if op.kind in ("d", "cc") and op.dprev > 0:
                        s = ("d", op.dsem)
                        if op.dprev > waits.get(s, 0):
                            waits[s] = op.dprev
                    for s, v in waits.items():
                        if waited.get(s, 0) >= v:
                            continue
                        waited[s] = v
                        sem = esem[s[1]] if s[0] == "e" else dsem[s[1]]
                        eobj.wait_ge(sem, v)
                    ins = op.fn(eobj)
                    if op.kind == "c":
                        if op.flag:
                            ins.then_inc(esem[op.eng], 1)
                    elif op.kind == "d":
                        ins.then_inc(dsem[op.dsem], 16)
                    else:
                        ins.then_inc(dsem[op.dsem], 1)
                if engname == "sp":
                    for s, v in final.items():
                        eobj.wait_ge(dsem[s], v)

            @block.tensor
            def _(e):
                run("pe", e)

            @block.scalar
            def _(e):
                run("act", e)

            @block.vector
            def _(e):
                run("dve", e)

            @block.gpsimd
            def _(e):
                run("pool", e)

            @block.sync
            def _(e):
                run("sp", e)
        return cnt, qcount


class B:
    def __init__(self, S):
        self.S = S
        self.evq = 0

    def mm(self, out, lhsT, rhs, start=True, stop=True, skip=False):
        if skip:
            self.S.c("pe", lambda e: e.matmul(out, lhsT, rhs, start=start, stop=stop, skip_group_check=True),
                     [lhsT, rhs], [out])
        else:
            self.S.c("pe", lambda e: e.matmul(out, lhsT, rhs, start=start, stop=stop), [lhsT, rhs], [out])

    def tr(self, out, in_, ident):
        self.S.c("pe", lambda e: e.transpose(out, in_, ident), [in_, ident], [out])

    def act(self, out, in_, func, scale=None, bias=None, extra=()):
        kw = {}
        rd = [in_] + list(extra)
        if scale is not None:
            kw["scale"] = scale
            if not isinstance(scale, (int, float)):
                rd.append(scale)
        if bias is not None:
            kw["bias"] = bias
            if not isinstance(bias, (int, float)):
                rd.append(bias)
        self.S.c("act", lambda e: e.activation(out, in_, func, **kw), rd, [out])

    def tt(self, out, a, b, op, eng="dve"):
        self.S.c(eng, lambda e: e.tensor_tensor(out, a, b, op), [a, b], [out])

    def stt(self, out, in0, scalar, in1, op0, op1, eng="dve"):
        rd = [in0, in1]
        if not isinstance(scalar, (int, float)):
            rd.append(scalar)
        self.S.c(eng, lambda e: e.scalar_tensor_tensor(out, in0, scalar, in1, op0, op1), rd, [out])

    def ts(self, out, in0, s1, s2, op0, op1=None, eng="dve"):
        rd = [in0]
        for s in (s1, s2):
            if s is not None and not isinstance(s, (int, float)):
                rd.append(s)
        if op1 is None:
            self.S.c(eng, lambda e: e.tensor_scalar(out, in0, s1, None, op0), rd, [out])
        else:
            self.S.c(eng, lambda e: e.tensor_scalar(out, in0, s1, s2, op0, op1), rd, [out])

    def copy(self, out, in_, eng="dve"):
        if eng == "act":
            self.S.c("act", lambda e: e.copy(out, in_), [in_], [out])
        else:
            self.S.c(eng, lambda e: e.tensor_copy(out, in_), [in_], [out])

    def evac(self, out, in_):
        self.evq += 1
        self.copy(out, in_, eng="act" if self.evq % 2 else "dve")

    def recip(self, out, in_, extra=()):
        self.S.c("dve", lambda e: e.reciprocal(out, in_), [in_] + list(extra), [out])

    def memset(self, ap, val, eng="dve"):
        self.S.c(eng, lambda e: e.memset(ap, val), [], [ap])

    def reduce_sum(self, out, in_):
        self.S.c("dve", lambda e: e.reduce_sum(out, in_, axis=mybir.AxisListType.X), [in_], [out])


def dap(t, offset, dims):
    return bass.AP(t, int(offset), [[int(s), int(c)] for s, c in dims])


def build_program(STAGE=99):
    nc = bass.Bass("TRN2", target_bir_lowering=False)
    S = Sched(nc)
    b = B(S)

    def finish():
        stats = S.emit()
        return nc, stats, len(S.ops)
    EI, EO = "ExternalInput", "ExternalOutput"
    x_d = S.dram("x", [NT, D], F32, EI)
    xh_d = S.dram("xh", [NH, D], F32, EI)
    mem_d = S.dram("mem", [256, D], F32, EI)
    ck_d = S.dram("ck", [PAST, 512], F32, EI)
    cv_d = S.dram("cv", [PAST, 512], F32, EI)
    cmk_d = S.dram("cmk", [2, 256, 512], F32, EI)
    cmv_d = S.dram("cmv", [2, 256, 512], F32, EI)
    spool_d = S.dram("spool", [15, 512], F32, EI)
    w_in_d = S.dram("w_in", [2, D, D], F32, EI)
    w_out_d = S.dram("w_out", [2, D, D], F32, EI)
    w_mem_d = S.dram("w_mem_kv", [2, D, D], F32, EI)
    w_ff1_d = S.dram("w_ff1", [2, D, 4 * D], F32, EI)
    w_ff2_d = S.dram("w_ff2", [2, 4 * D, D], F32, EI)
    w_pool_d = S.dram("w_pool", [4, 128, 128], F32, EI)
    w_kv_d = S.dram("w_kv", [D, D], F32, EI)
    lq_d = S.dram("lq", [1, 256], F32, EI)
    cols_d = S.dram("cols", [128, 69], F32, EI)
    qaug_d = S.dram("qaug", [3, 4, NT], BF16, EI)
    kaug_d = S.dram("kaug", [3, 512], BF16, EI)
    dtab_d = S.dram("dtab", [128, 4 * 4 * 128], F32, EI)
    dtabs_d = S.dram("dtabs", [32, 4 * 32], F32, EI)
    rc16_d = S.dram("rc16", [1, 64], F32, EI)
    ident_d = S.dram("identd", [128, 128], F32, EI)

    y_d = S.dram("y", [NT, D], F32, EO)
    memk_d = S.dram("memk", [2, 256, 512], F32, EO)
    memv_d = S.dram("memv", [2, 256, 512], F32, EO)
    poolo_d = S.dram("poolo", [2, 15, 512], F32, EO)
    ko_d = S.dram("ko", [NT, 512], F32, EO)
    vo_d = S.dram("vo", [NT, 512], F32, EO)

    xk_in = [S.dram("xk_in%d" % i, [512, 1024], BF16) for i in range(2)]
    xk_out = [S.dram("xk_out%d" % i, [2048, 1024], BF16) for i in range(2)]
    xv_in = [S.dram("xv_in%d" % i, [256, 2048], BF16) for i in range(2)]
    xv_out = [S.dram("xv_out%d" % i, [1024, 2048], BF16) for i in range(2)]

    XT = S.sbuf("XT", [128, 8, NT], F32)
    RW = S.reserve(46080)
    WSQ = [S.sbuf("WSQ%d" % i, [128, 8, 1024], BF16) for i in range(2)]
    WM = S.reserve(32768)
    MKpL = [S.sbuf("MKp%d" % i, [128, 4, 256], BF16) for i in range(2)]
    MVpL = [S.sbuf("MVp%d" % i, [128, 2, 512], BF16) for i in range(2)]
    MKs = S.sbuf("MKs", [128, 4, 256], BF16)
    MVs = S.sbuf("MVs", [128, 2, 512], BF16)
    DTAB = S.sbuf("DTAB", [128, 4, 4, 128], BF16)
    DTABS = S.sbuf("DTABS", [32, 4, 32], BF16)
    TF = [S.sbuf("TF%d" % i, [128, 512], F32) for i in range(4)]
    SQ = [S.sbuf("SQ%d" % i, [128, 512], BF16) for i in range(3)]
    IDF = S.sbuf("IDF", [128, 128], F32)
    IDB = S.sbuf("IDB", [128, 128], BF16)
    ONES = S.sbuf("ONES", [128, 128], BF16)
    OND1 = S.sbuf("OND1", [128, 128], BF16)
    OND2 = S.sbuf("OND2", [128, 128], BF16)
    COLS = S.sbuf("COLS", [128, 72], F32)
    LQ = S.sbuf("LQ", [128, 256], F32)
    LT = S.sbuf("LT", [128, 64], F32)
    LS = S.sbuf("LS", [128, 8], F32)
    RC16 = S.sbuf("RC16", [128, 4, 16], F32)
    WPOOL = S.sbuf("WPOOL", [128, 4, 128], BF16)
    KSN0 = S.sbuf("KSN0", [128, 4, 32], BF16)
    KSN1 = S.sbuf("KSN1", [128, 4, 32], BF16)
    VSN = S.sbuf("VSN", [32, 512], BF16)
    SMALL = S.sbuf("SMALL", [128, 4, 16], F32)

    XNT = S.sbuf("XNT", [128, 8, 512], BF16, at=RW + 0)
    TM = S.sbuf("TM", [128, 8, 512], BF16, at=RW + 0)
    U = S.sbuf("U", [128, 4, 4, 144], F32, at=RW + 8192)
    UH = S.sbuf("UH", [128, 4, 16, 16], F32, at=RW + 17408)
    PA = S.sbuf("PA", [128, 4, 4, 144], F32, at=RW + 21504)
    PB = S.sbuf("PB", [128, 3, 4, 144], F32, at=RW + 30720)
    DD = S.sbuf("DD", [128, 4, 512], BF16, at=RW + 37632)
    QM = S.sbuf("QM", [128, 4, 512], BF16, at=RW + 41984)
    TMm1 = S.sbuf("TMm1", [128, 4, 512], BF16, at=RW + 37888)
    TMt1 = S.sbuf("TMt1", [128, 4, 512], BF16, at=WM + 28672)
    XNTB = S.sbuf("XNTB", [128, 8, 512], BF16, at=RW + 20480)
    Q0 = S.sbuf("Q0", [128, 4, 512], BF16, at=RW + 8192)
    Q1 = S.sbuf("Q1", [128, 4, 512], BF16, at=RW + 12288)
    OA = S.sbuf("OA", [128, 512], F32, at=RW + 16384)
    OB = S.sbuf("OB", [128, 512], F32, at=RW + 18432)
    XN = S.sbuf("XN", [128, 8, NT], BF16, at=RW + 0)
    HQ = [S.sbuf("HQ%d" % i, [128, 4, 512], BF16, at=RW + 33280 + 4096 * i) for i in range(2)]
    RL = [S.sbuf("RL%d" % i, [128, 512], F32, at=RW + 41472 + 2048 * i) for i in range(2)]
    XHT = S.sbuf("XHT", [128, 8, 256], F32, at=RW + 21504)
    XNH = S.sbuf("XNH", [128, 8, 256], BF16, at=RW + 29696)
    MEMT = S.sbuf("MEMT", [128, 8, 256], F32, at=RW + 0)
    MEMN = S.sbuf("MEMN", [128, 8, 256], BF16, at=RW + 8192)
    YN = S.sbuf("YN", [128, 8, 512], F32, at=RW + 0)
    WM1 = [S.sbuf("WM1_%d" % i, [128, 8, 512], BF16, at=WM + 8192 * i) for i in range(2)]
    WM2 = [S.sbuf("WM2_%d" % i, [128, 4, 1024], BF16, at=WM + 16384 + 8192 * i) for i in range(2)]
    KB0 = [S.sbuf("KB0_%d" % i, [128, 512], BF16, at=WM + 1024 * i) for i in range(NKBUF)]
    KB1 = [S.sbuf("KB1_%d" % i, [128, 512], BF16, at=WM + 4096 + 1024 * i) for i in range(NKBUF)]
    VB = [S.sbuf("VB_%d" % i, [128, 4, 128], BF16, at=WM + 8192 + 1024 * i) for i in range(NKBUF)]
    EB = [S.sbuf("EB_%d" % i, [128, 512], BF16, at=WM + 12288 + 1024 * i) for i in range(8)]
    CKS = S.sbuf("CKS", [128, 4, 512], F32, at=WM + 20480)
    CVSF = S.sbuf("CVSF", [128, 4, 512], F32, at=RW + 21504)
    NEB = 6
    EBP = [S.sbuf("EBP%d" % i, [128, 2, 512], BF16, at=WM + 12288 + 2048 * i) for i in range(NEB)]
    ESP = [S.sbuf("ESP%d" % i, [128, 2, 512], BF16, at=WM + 24576 + 2048 * i) for i in range(2)]
    ACC = S.sbuf("ACC", [128, 2, 512], F32, at=RW + 29696)
    HI = S.sbuf("HI", [128, 2, 512], BF16, at=RW + 33792)
    LO = S.sbuf("LO", [128, 2, 512], BF16, at=RW + 35840)
    XIN = S.sbuf("XIN", [128, 4, 1024], F32, at=WM + 0)
    XIN2 = S.sbuf("XIN2", [128, 4, 1024], F32, at=WM + 16384)
    STG = [S.sbuf("STG%d" % i, [128, 1024], F32, at=WM + 16384 + 4096 * i) for i in range(2)]
    VST = [S.sbuf("VST%d" % i, [128, 512], BF16, at=WM + 24576 + 1024 * i) for i in range(2)]
    KST = [S.sbuf("KST%d" % i, [128, 512], BF16, at=WM + 26624 + 1024 * i) for i in range(2)]
    MST = S.sbuf("MST", [128, 2, 1024], F32, at=WM + 0)
    CMS = S.sbuf("CMS", [128, 2, 512], F32, at=WM + 8192)

    PP = [S.psum("PP%d" % i, (128, 1024)) for i in range(4)]
    PS = [PP[i // 2][:, (i % 2) * 512:(i % 2) * 512 + 512] for i in range(8)]
    psr = [0]

    def bank():
        psr[0] += 1
        return PS[psr[0] % 8]

    def gcol(i):
        return COLS[:, i:i + 1]

    S.dma("sp", IDF[:], ident_d[:, :])
    S.dma("sp", COLS[:, 0:69], cols_d[:, :])
    S.dma("sp", LQ[:], dap(lq_d, 0, [(0, 128), (1, 256)]))
    S.dma("sp", RC16[:], dap(rc16_d, 0, [(0, 128), (16, 4), (1, 16)]))
    def load_sq(dst, src_t, off):
        for kc in range(8):
            S.dma("pool", dst[:, kc, :], dap(src_t, off + kc * 128 * 1024, [(1024, 128), (1, 1024)]))

    load_sq(WSQ[0], w_in_d, 0)
    load_sq(WSQ[1], w_mem_d, 0)
    S.dma("pool", WPOOL[:], dap(w_pool_d, 0, [(128, 128), (16384, 4), (1, 128)]))
    S.dma("pool", DTAB[:], dap(dtab_d, 0, [(2048, 128), (512, 4), (128, 4), (1, 128)]))
    S.dma("pool", DTABS[:], dap(dtabs_d, 0, [(128, 32), (32, 4), (1, 32)]))
    b.copy(IDB[:], IDF[:])
    b.memset(ONES[:], 1.0)
    b.memset(OND1[:], 1.0 / 1024)
    b.memset(OND2[:], 1.0 / 128)
    b.tt(LT[:], LQ[:, 0:64], LQ[:, 64:128], ALU.mult)
    b.reduce_sum(LS[:, 0:1], LT[:])
    b.tt(LT[:], LQ[:, 128:192], LQ[:, 192:256], ALU.mult)
    b.reduce_sum(LS[:, 1:2], LT[:])
    b.act(LS[:, 4:6], LS[:, 0:2], ACTF.Exp)
    b.tt(LS[:, 6:7], LS[:, 5:6], LS[:, 4:5], ALU.subtract)
    b.ts(LS[:, 2:3], LS[:, 6:7], -LAM_I, None, ALU.add)
    b.ts(LS[:, 3:4], gcol(68), 1.0 - LAM_I, None, ALU.mult)
    NEGLAM = LS[:, 2:3]
    GSUB = LS[:, 3:4]

    sqr = [0]

    def norm(src, dst, n, g0, invn_ones, nchunks=8):
        ps = bank()
        for c in range(nchunks):
            sq = SQ[sqr[0] % 3]
            sqr[0] += 1
            b.act(sq[:, :n], src(c), ACTF.Square)
            b.mm(ps[:, :n], invn_ones[:], sq[:, :n], start=(c == 0), stop=(c == nchunks - 1))
        b.act(TF[0][:, :n], ps[:, :n], ACTF.Ln, bias=EPS)
        b.act(TF[1][:, :n], TF[0][:, :n], ACTF.Exp, scale=-0.5)
        for c in range(nchunks):
            b.stt(dst(c), src(c), gcol(g0 + c), TF[1][:, :n], ALU.mult, ALU.mult)

    for t, (c0, n) in enumerate(TILES):
        if n == 512:
            xin = XIN if t % 2 == 0 else XIN2
            S.dma("sp", xin[:], dap(x_d, c0 * D, [(D, 128), (128 * D, 4), (1, D)]))
            for c in range(8):
                ps = bank()
                for blk in range(4):
                    b.tr(ps[:, blk * 128:(blk + 1) * 128], xin[:, blk, c * 128:(c + 1) * 128], IDF[:])
                b.evac(XT[:, c, c0:c0 + 512], ps[:, :])
        else:
            S.dma("sp", XIN[0:32, 0, :], dap(x_d, c0 * D, [(D, 32), (1, D)]))
            ps = bank()
            for c in range(8):
                b.tr(ps[:, c * 32:(c + 1) * 32], XIN[0:32, 0, c * 128:(c + 1) * 128], IDF[0:32, 0:32])
            b.evac(XT[:, :, c0:c0 + 32], ps[:, 0:256].rearrange("p (c t) -> p c t", c=8))
    S.dma("sp", XIN[:, 0:2, :], dap(xh_d, 0, [(D, 128), (128 * D, 2), (1, D)]))
    for c in range(8):
        ps = bank()
        for blk in range(2):
            b.tr(ps[:, blk * 128:(blk + 1) * 128], XIN[:, blk, c * 128:(c + 1) * 128], IDF[:])
        b.evac(XHT[:, c, :], ps[:, 0:256])

    def mem_kv_prompt(l, first, W):
        if first:
            S.dma("sp", MST[:], dap(mem_d, 0, [(D, 128), (128 * D, 2), (1, D)]))
            for c in range(8):
                ps = bank()
                for mb in range(2):
                    b.tr(ps[:, mb * 128:(mb + 1) * 128], MST[:, mb, c * 128:(c + 1) * 128], IDF[:])
                b.evac(MEMT[:, c, :], ps[:, 0:256])
        norm(lambda c: MEMT[:, c, :], lambda c: MEMN[:, c, :], 256, 16 + 8 * l, OND1)
        MKp, MVp = MKpL[l], MVpL[l]
        for h in range(4):
            ps = bank()
            for kc in range(8):
                b.mm(ps[:, 0:256], W[:, kc, h * 128:(h + 1) * 128], MEMN[:, kc, :], start=(kc == 0), stop=(kc == 7))
            b.evac(MKp[:, h, :], ps[:, 0:256])
        for mb in range(2):
            stg = STG[mb]
            for half in range(2):
                ps = bank()
                for kc in range(8):
                    b.mm(ps[:, :], MEMN[:, kc, mb * 128:(mb + 1) * 128], W[:, kc, half * 512:(half + 1) * 512],
                         start=(kc == 0), stop=(kc == 7))
                b.evac(stg[:, half * 512:(half + 1) * 512], ps[:, :])
            b.copy(MVp[:, mb, :], stg[:, 512:1024], eng="act")
            S.dma("sp", dap(memk_d, l * 256 * 512 + mb * 128 * 512, [(512, 128), (1, 512)]), stg[:, 0:512])
            S.dma("sp", dap(memv_d, l * 256 * 512 + mb * 128 * 512, [(512, 128), (1, 512)]), stg[:, 512:1024])

    def mem_kv_sample(l):
        S.dma("sp", CMS[:], dap(cmk_d, l * 256 * 512, [(512, 128), (128 * 512, 2), (1, 512)]))
        S.dma("pool", MVs[:], dap(cmv_d, l * 256 * 512, [(512, 128), (128 * 512, 2), (1, 512)]))
        for h in range(4):
            ps = bank()
            for mb in range(2):
                b.tr(ps[:, mb * 128:(mb + 1) * 128], CMS[:, mb, h * 128:(h + 1) * 128], IDF[:])
            b.evac(MKs[:, h, :], ps[:, 0:256])

    ebr = [0]

    def mem_attend(n, MK, MV):
        for h in range(4):
            es = []
            for mb in range(2):
                ps = bank()
                b.mm(ps[:, :n], MK[:, h, mb * 128:(mb + 1) * 128], QM[:, h, :n])
                e = SQ[sqr[0] % 3]
                sqr[0] += 1
                b.act(e[:, :n], ps[:, :n], ACTF.Exp, scale=128.0 ** -0.5)
                es.append(e)
            po = bank()
            pd = bank()
            for mb in range(2):
                b.mm(po[:, :n], MV[:, mb, h * 128:(h + 1) * 128], es[mb][:, :n], start=(mb == 0), stop=(mb == 1))
            for mb in range(2):
                b.mm(pd[:, :n], ONES[:], es[mb][:, :n], start=(mb == 0), stop=(mb == 1))
            b.act(TF[2][:, :n], pd[:, :n], ACTF.Ln)
            b.act(TF[2][:, :n], TF[2][:, :n], ACTF.Exp, scale=-1.0)
            b.tt(TMV[4 + h][:, :n], po[:, :n], TF[2][:, :n], ALU.mult)

    def proj(W, ocs, rhs, n, evac_fn):
        for oc in ocs:
            ps = bank()
            for kc in range(8):
                b.mm(ps[:, :n], W[:, kc, oc * 128:(oc + 1) * 128], rhs(kc), start=(kc == 0), stop=(kc == 7))
            evac_fn(oc, ps)

    TMV = [None] * 8

    def wout_residual(W, c0, n, mid=None):
        def ev(oc, ps):
            b.tt(XT[:, oc, c0:c0 + n], ps[:, :n], XT[:, oc, c0:c0 + n], ALU.add)
        proj(W, range(4), lambda kc: TMV[kc][:, :n], n, ev)
        if mid is not None:
            mid()
        proj(W, range(4, 8), lambda kc: TMV[kc][:, :n], n, ev)

    def mlp(l, after_y=None):
        def load_group(hg):
            w1 = WM1[hg % 2]
            w2 = WM2[hg % 2]
            for kc in range(8):
                S.dma("pool", w1[:, kc, :],
                      dap(w_ff1_d, l * D * 4 * D + kc * 128 * 4 * D + hg * 512, [(4 * D, 128), (1, 512)]))
            for hb in range(4):
                S.dma("pool", w2[:, hb, :],
                      dap(w_ff2_d, l * 4 * D * D + (hg * 512 + hb * 128) * D, [(D, 128), (1, D)]))
        MT = [(416 * i, 416) for i in range(5)]
        units = [(hg, t) for hg in range(8) for t in range(len(MT))]
        load_group(0)
        load_group(1)

        def emit_h(u, k):
            hg, t = u
            c0, n = MT[t]
            hq = HQ[k % 2]
            for hb in range(4):
                ps = bank()
                for kc in range(8):
                    b.mm(ps[:, :n], WM1[hg % 2][:, kc, hb * 128:(hb + 1) * 128], XN[:, kc, c0:c0 + n],
                         start=(kc == 0), stop=(kc == 7))
                rl = RL[hb % 2]
                b.act(rl[:, :n], ps[:, :n], ACTF.Relu)
                b.act(hq[:, hb, :n], rl[:, :n], ACTF.Square)

        def emit_y(u, k):
            hg, t = u
            c0, n = MT[t]
            hq = HQ[k % 2]
            for oc in range(8):
                ps = bank()
                for hb in range(4):
                    b.mm(ps[:, :n], WM2[hg % 2][:, hb, oc * 128:(oc + 1) * 128], hq[:, hb, :n],
                         start=(hb == 0), stop=(hb == 3))
                b.tt(XT[:, oc, c0:c0 + n], ps[:, :n], XT[:, oc, c0:c0 + n], ALU.add)

        for k, u in enumerate(units):
            emit_h(u, k)
            if k >= 1:
                emit_y(units[k - 1], k - 1)
                pu = units[k - 1]
                if pu[1] == len(MT) - 1 and pu[0] + 2 < 8:
                    load_group(pu[0] + 2)
                if after_y is not None and pu[0] == 7:
                    after_y(MT[pu[1]])
        emit_y(units[-1], len(units) - 1)
        if after_y is not None:
            after_y(MT[-1])

    mem_kv_prompt(0, True, WSQ[1])
    mem_kv_sample(0)
    load_sq(WSQ[1], w_out_d, 0)

    if STAGE < 1:
        return finish()
    norm(lambda c: XHT[:, c, :], lambda c: XNH[:, c, :], 256, 0, OND1)

    def ev_halo(oc, ps):
        b.evac(UH[:, oc, :, :], ps[:, 0:256].rearrange("p (b t) -> p b t", b=16))
    proj(WSQ[0], range(4), lambda kc: XNH[:, kc, :], 256, ev_halo)


    def pool_tile(t, c0, n):
        nb = 4 if n == 512 else 1
        tw = 144 if n == 512 else 48
        if n == 512:
            b.copy(U[:, :, :, 0:16], UH[:, :, 4 * t:4 * t + 4, :])
        Uv = U[:, :, 0:nb, 0:tw]
        PAv = PA[:, :, 0:nb, 0:tw]
        PBv = PB[:, :, 0:nb, 0:tw]
        b.tt(PAv[:, :, :, 1:tw], Uv[:, :, :, 1:tw], Uv[:, :, :, 0:tw - 1], ALU.add)
        b.tt(PBv[:, :, :, 3:tw], PAv[:, 1:4, :, 3:tw], PAv[:, 1:4, :, 1:tw - 2], ALU.add)
        def dd(g, srcv, w):
            b.stt(DD[:, g, 0:n].rearrange("p (b t) -> p b t", b=nb), srcv, 1.0 / w,
                  U[:, g, 0:nb, 16:tw], ALU.mult, ALU.subtract)
            if t == 0:
                b.tt(SMALL[:, g, :], srcv[:, 0, 0:16], RC16[:, g, :], ALU.mult)
                b.tt(DD[:, g, 0:16], SMALL[:, g, :], U[:, g, 0, 16:32], ALU.subtract)
        dd(0, PA[:, 0, 0:nb, 16:tw], 2)
        dd(1, PB[:, 0, 0:nb, 16:tw], 4)
        b.tt(PAv[:, 2:4, :, 7:tw], PBv[:, 1:3, :, 7:tw], PBv[:, 1:3, :, 3:tw - 4], ALU.add)
        dd(2, PA[:, 2, 0:nb, 16:tw], 8)
        b.tt(PBv[:, 0:1, :, 15:tw], PAv[:, 3:4, :, 15:tw], PAv[:, 3:4, :, 7:tw - 8], ALU.add)
        dd(3, PB[:, 0, 0:nb, 16:tw], 16)

    def pool_mm(n):
        for g in range(4):
            ps = bank()
            b.mm(ps[:, :n], WPOOL[:, g, :], DD[:, g, 0:n])
            b.act(TMV[g][:, :n], ps[:, :n], ACTF.Copy, scale=gcol(64 + g))

    def pool_out(slot, blk, col0):
        ps = bank()
        for g in range(4):
            b.tr(ps[0:15, g * 128:(g + 1) * 128], U[:, g, blk, col0:col0 + 15], IDF[:])
        b.evac(TF[3][0:15, :], ps[0:15, :])
        S.dma("sp", dap(poolo_d, slot * 15 * 512, [(512, 15), (1, 512)]), TF[3][0:15, :])

    XNT0 = [S.sbuf("XNT0_%d" % i, [128, 8, 512], BF16, at=WM + 8192 * i) for i in range(2)]
    TM0 = S.sbuf("TM0", [128, 8, 512], BF16, at=WM + 16384)
    for kc in range(8):
        TMV[kc] = TM0[:, kc, :]
    norm(lambda c: XT[:, c, 0:512], lambda c: XNT0[0][:, c, :512], 512, 0, OND1)
    for t, (c0, n) in enumerate(TILES):
        samp = n != 512
        xn = XNT0[t % 2]
        if samp:
            S.dma("sp", TF[3][0:15, :], dap(spool_d, 0, [(512, 15), (1, 512)]))
            ps = bank()
            for g in range(4):
                b.tr(ps[:, g * 16:g * 16 + 15], TF[3][0:15, g * 128:(g + 1) * 128], IDF[0:15, 0:15])
            b.evac(U[:, :, 0, 1:16], ps[:, 0:64].rearrange("p (g t) -> p g t", g=4)[:, :, 0:15])

        def ev0(oc, ps, n=n, samp=samp):
            if oc < 4:
                if not samp:
                    b.evac(U[:, oc, :, 16:144], ps[:, :].rearrange("p (b t) -> p b t", b=4))
                else:
                    b.evac(U[:, oc, 0, 16:48], ps[:, 0:32])
            else:
                b.evac(QM[:, oc - 4, :n], ps[:, :n])
        proj(WSQ[0], range(8), lambda kc: xn[:, kc, :n], n, ev0)
        pool_tile(t, c0, n)
        mem_attend(n, MKs if samp else MKpL[0], MVs if samp else MVpL[0])
        pool_mm(n)
        mid0 = None
        if t + 1 < len(TILES):
            def mid0(t=t):
                c1, n1 = TILES[t + 1]
                xn1 = XNT0[(t + 1) % 2]
                norm(lambda c: XT[:, c, c1:c1 + n1], lambda c: xn1[:, c, :n1], n1, 0, OND1)
        if t == 3:
            pool_out(0, 3, 129)
        if samp:
            pool_out(1, 0, 33)
        wout_residual(WSQ[1], c0, n, mid=mid0)
    for kc in range(4):
        TMV[kc] = TMt1[:, kc, :]
        TMV[4 + kc] = TMm1[:, kc, :]

    if STAGE < 2:
        return finish()
    load_sq(WSQ[0], w_mem_d, D * D)
    mem_kv_prompt(1, True, WSQ[0])
    for t, (c0, n) in enumerate(TILES):
        norm(lambda c: XT[:, c, c0:c0 + n], lambda c: XN[:, c, c0:c0 + n], n, 32, OND1)
    load_sq(WSQ[0], w_kv_d, 0)
    load_sq(WSQ[1], w_in_d, D * D)
    def kv_norm(tile):
        c0, n = tile
        norm(lambda c: XT[:, c, c0:c0 + n], lambda c: XN[:, c, c0:c0 + n], n, 48, OND1)
    mlp(0, after_y=kv_norm)

    if STAGE < 3:
        return finish()
    mem_kv_sample(1)
    kq = [0]

    def exchange(pairs):
        for (ti, to) in pairs:
            S.collective("pool", lambda e, ti=ti, to=to: e.collective_compute(
                "AllGather", ALU.bypass, replica_groups=[[0, 1, 2, 3], [4, 5, 6, 7]],
                ins=[ti.ap().opt()], outs=[to.ap().opt()]), [ti.ap()], [to.ap()])

    def kv_kt(t):
        c0, n = TILES[t]
        samp = n != 512

        def evk(oc, ps):
            if samp:
                b.copy(KSN0[0:64, oc, :], ps[0:64, 0:32], eng="act")
                b.copy(KSN1[64:128, oc, :], ps[64:128, 0:32], eng="dve")
            else:
                ks = KST[kq[0] % 2]
                kq[0] += 1
                b.evac(ks[:, :], ps[:, :])
                S.dma("sp", dap(xk_in[t // 2], oc * 128 * 1024 + (c0 % 1024), [(1024, 128), (1, 512)]), ks[:, :])
        proj(WSQ[0], range(4), lambda kc: XN[:, kc, c0:c0 + n], n, evk)

    def kv_tok(tb):
        r0 = tb * 128
        rows = 128 if tb < 16 else 32
        stg = STG[tb % 2]
        for half in range(2):
            ps = bank()
            for kc in range(8):
                b.mm(ps[0:rows, :], XN[:, kc, r0:r0 + rows], WSQ[0][:, kc, half * 512:(half + 1) * 512],
                     start=(kc == 0), stop=(kc == 7))
            b.evac(stg[0:rows, half * 512:(half + 1) * 512], ps[0:rows, :])
        S.dma("sp", dap(ko_d, r0 * 512, [(512, rows), (1, 512)]), stg[0:rows, 0:512])
        S.dma("sp", dap(vo_d, r0 * 512, [(512, rows), (1, 512)]), stg[0:rows, 512:1024])
        if tb < 16:
            vs = VST[tb % 2]
            b.copy(vs[:, :], stg[:, 512:1024], eng="act")
            S.dma("sp", dap(xv_in[tb // 8], (r0 % 1024) * 512, [(512, 128), (1, 512)]), vs[:, :])
        else:
            b.copy(VSN[:, :], stg[0:32, 512:1024], eng="act")

    kv_kt(0)
    kv_kt(1)
    for tb in range(8):
        kv_tok(tb)
    exchange([(xk_in[0], xk_out[0]), (xv_in[0], xv_out[0])])
    kv_kt(2)
    kv_kt(3)
    kv_kt(4)
    for tb in range(8, 17):
        kv_tok(tb)
    load_sq(WSQ[0], w_out_d, D * D)
    exchange([(xk_in[1], xk_out[1]), (xv_in[1], xv_out[1])])

    if STAGE < 4:
        return finish()

    for i in range(NKBUF):
        b.memset(KB1[i][0:64, :], 0.0)
        S.dma("sp", KB0[i][64:67, :], dap(kaug_d, 0, [(512, 3), (1, 512)]))
        S.dma("sp", KB1[i][0:3, :], dap(kaug_d, 0, [(512, 3), (1, 512)]))
    b.memset(KSN1[0:64, :, :], 0.0)
    for h in range(4):
        S.dma("sp", KSN0[64:67, h, :], dap(kaug_d, 0, [(512, 3), (1, 32)]))
        S.dma("sp", KSN1[0:3, h, :], dap(kaug_d, 0, [(512, 3), (1, 32)]))
    b.memset(Q1[0:64, :, :], 0.0)

    kbr = [0]

    def finish_head(h, n, po0, po1, pd0, pd1, extra=(), lnbank=None):
        b.act(TF[2][:, :n], pd0, ACTF.Ln, extra=extra)
        b.act(TF[2][:, :n], TF[2][:, :n], ACTF.Exp, scale=-1.0)
        b.tt(OA[:, :n], po0, TF[2][:, :n], ALU.mult)
        b.act(TF[3][:, :n], pd1, ACTF.Ln)
        b.act(TF[3][:, :n], TF[3][:, :n], ACTF.Exp, scale=-1.0)
        b.tt(OB[:, :n], po1, TF[3][:, :n], ALU.mult)
        b.stt(OA[:, :n], OB[:, :n], NEGLAM, OA[:, :n], ALU.mult, ALU.add)
        sq = SQ[sqr[0] % 3]
        sqr[0] += 1
        b.act(sq[:, :n], OA[:, :n], ACTF.Square)
        ps = lnbank if lnbank is not None else bank()
        b.mm(ps[:, :n], OND2[:], sq[:, :n])
        b.act(TF[0][:, :n], ps[:, :n], ACTF.Ln, bias=EPS)
        b.act(TF[1][:, :n], TF[0][:, :n], ACTF.Exp, scale=-0.5)
        b.stt(TMV[h][:, :n], OA[:, :n], GSUB, TF[1][:, :n], ALU.mult, ALU.mult)

    def diff_attend_prompt(m):
        n = 512
        pending = [None]
        for h in range(4):
            items = []
            past = [(r, g) for g in range(m) for r in range(4)]
            diag = [(r, m) for r in range(4)]
            units = []
            kq_ = len(past) // 4
            for q in range(4):
                units += past[q * kq_:(q + 1) * kq_] + [diag[q]]
            SP = [PP[0], PP[1], PP[2]]
            PO0, PO1 = PS[6], PS[7]
            loaded = {}

            def load_unit(ui, h=h):
                r, g = units[ui]
                k = kbr[0] % NKBUF
                kbr[0] += 1
                kt = xk_out[g // 2]
                krow = r * 512 + h * 128
                S.dma("sp", KB0[k][0:64, :], dap(kt, krow * 1024 + 512 * (g % 2), [(1024, 64), (1, 512)]))
                S.dma("sp", KB1[k][64:128, :], dap(kt, (krow + 64) * 1024 + 512 * (g % 2), [(1024, 64), (1, 512)]))
                S.dma("sp", VB[k][:, :, :], dap(xv_out[g // 2], r * 256 * 2048 + 512 * (g % 2) * 512 + h * 128,
                                                 [(512, 128), (128 * 512, 4), (1, 128)]))
                loaded[ui] = k
            for ui, (r, g) in enumerate(units):
                for u in range(4):
                    kb = 4 * (4 * g + u) + r
                    if g < m:
                        items.append((ui, u, kb, 0, None))
                    else:
                        items.append((ui, u, kb, u * 128, r))
            nit = len(items)
            for ui in range(min(NKBUF - 1, len(units))):
                load_unit(ui)
            nxt = min(NKBUF - 1, len(units))
            hist = {}
            e0 = None
            for i in range(nit + 2):
                cur = None
                if i < nit:
                    ui, u, kb, lo, r = items[i]
                    k = loaded[ui]
                    bias = SLOPES[h] * 128.0 * kb
                    sp = SP[i % 3]
                    for c in range(2):
                        off = c * 512
                        KBc = KB0[k][0:67, u * 128:(u + 1) * 128] if c == 0 else KB1[k][:, u * 128:(u + 1) * 128]
                        Qc = Q0[0:67, h, lo:n] if c == 0 else Q1[:, h, lo:n]
                        b.mm(sp[:, off + lo:off + n], KBc, Qc, start=True, stop=(r is None))
                        if r is not None:
                            b.mm(sp[:, off + lo:off + lo + 128], IDB[:], DTAB[:, h, r, :], start=False, stop=True)
                    e = EBP[ebr[0] % NEB]
                    ebr[0] += 1
                    spv = sp[:, :].rearrange("p (c t) -> p c t", c=2)
                    b.act(e[:, :, lo:n], spv[:, :, lo:n], ACTF.Exp, bias=bias)
                    esb = ESP[ui % 2]
                    if u == 0:
                        e0 = e
                    elif u == 1:
                        if lo > 0:
                            b.copy(esb[:, :, 0:lo], e0[:, :, 0:lo], eng="dve")
                        b.tt(esb[:, :, lo:n], e0[:, :, lo:n], e[:, :, lo:n], ALU.add, eng="pool")
                    else:
                        b.tt(esb[:, :, lo:n], esb[:, :, lo:n], e[:, :, lo:n], ALU.add)
                        if u == 3 and ui % 2 == 1:
                            b.tt(esb[:], esb[:], ESP[(ui - 1) % 2][:], ALU.add)
                            if ui == 1:
                                b.copy(ACC[:], esb[:])
                            else:
                                b.tt(ACC[:], ACC[:], esb[:], ALU.add)
                    hist[i] = (k, u, lo, e, ui)
                if i - 2 >= 0:
                    pk, pu, plo, pe_, pui = hist.pop(i - 2)
                    first = (i - 2 == 0)
                    last = (i - 2 == nit - 1)
                    b.mm(PO0[:, plo:n], VB[pk][:, pu, :], pe_[:, 0, plo:n], start=first, stop=last)
                    b.mm(PO1[:, plo:n], VB[pk][:, pu, :], pe_[:, 1, plo:n], start=first, stop=last)
                if i < nit and items[i][1] == 2 and nxt < len(units):
                    load_unit(nxt)
                    nxt += 1
                if i == 6 and pending[0] is not None:
                    pending[0]()
                    pending[0] = None
            b.copy(OA[:, :n], PO0[:, :n], eng="act")
            b.copy(OB[:, :n], PO1[:, :n], eng="dve")

            b.copy(HI[:], ACC[:])
            b.tt(LO[:], ACC[:], HI[:], ALU.subtract)

            def epilogue(h=h):
                pd = [PS[4], PS[5]]
                for c in range(2):
                    b.mm(pd[c][:, :n], ONES[:], HI[:, c, :], start=True, stop=False)
                    b.mm(pd[c][:, :n], ONES[:], LO[:, c, :], start=False, stop=True)
                finish_head(h, n, OA[:, :n], OB[:, :n], pd[0][:, :n], pd[1][:, :n], lnbank=PS[4])
            if h < 3:
                pending[0] = epilogue
            else:
                epilogue()

    def diff_attend_sample():
        n = 32
        POD = PS[4]
        PT = [PS[0], PS[1]]
        PSS = [PS[2], PS[3]]
        EBS = [S.sbuf("EBS%d" % i, [128, 256], BF16, at=WM + 12288 + 512 * i) for i in range(4)]
        units = [(g, h) for g in range(4) for h in range(4)]
        kmap = {}

        def prep(ni):
            g, h = units[ni]
            if h == 0:
                S.dma("sp", CKS[:], dap(ck_d, g * 512 * 512, [(512, 128), (128 * 512, 4), (1, 512)]))
                S.dma("sp", CVSF[:], dap(cv_d, g * 512 * 512, [(512, 128), (128 * 512, 4), (1, 512)]))
            k = kbr[0] % NKBUF
            kbr[0] += 1
            kmap[ni] = k
            b.copy(VB[k][:, :, :], CVSF[:, :, h * 128:(h + 1) * 128], eng="dve")
            ps = PT[ni % 2]
            for u in range(4):
                b.tr(ps[:, u * 128:(u + 1) * 128], CKS[:, u, h * 128:(h + 1) * 128], IDF[:])
            b.copy(KB0[k][0:64, :], ps[0:64, :], eng="act")
            b.copy(KB1[k][64:128, :], ps[64:128, :], eng="dve")

        def scores(ni):
            g, h = units[ni]
            k = kmap[ni]
            ps = PSS[ni % 2]
            e = EBS[ni % 4]
            for u in range(4):
                for c in range(2):
                    o = (u * 2 + c) * 32
                    b.mm(ps[:, o:o + n], KB0[k][0:67, u * 128:(u + 1) * 128] if c == 0 else KB1[k][:, u * 128:(u + 1) * 128],
                         Q0[0:67, h, 0:n] if c == 0 else Q1[:, h, 0:n])
            for u in range(4):
                b.act(e[:, u * 64:(u + 1) * 64], ps[:, u * 64:(u + 1) * 64], ACTF.Exp,
                      bias=SLOPES[h] * 128.0 * (4 * g + u), extra=[ps[:, 0:256]])

        def av(ni):
            g, h = units[ni]
            k = kmap[ni]
            e = EBS[ni % 4]
            for u in range(4):
                for c in range(2):
                    o = (u * 2 + c) * 32
                    o0 = (h * 2 + c) * 32
                    b.mm(POD[:, o0:o0 + n], VB[k][:, u, :], e[:, o:o + n], start=False, stop=False, skip=True)
                    b.mm(POD[:, 256 + o0:256 + o0 + n], ONES[:], e[:, o:o + n], start=False, stop=False, skip=True)

        b.memset(POD[:, :], 0.0)
        prep(0)
        prep(1)
        for ni in range(len(units)):
            scores(ni)
            if ni + 2 < len(units):
                prep(ni + 2)
            av(ni)
        for h in range(4):
            ps = PSS[h % 2]
            e = EBS[h % 4]
            for c in range(2):
                o = c * 32
                b.mm(ps[0:32, o:o + n], KSN0[0:67, h, :] if c == 0 else KSN1[:, h, :],
                     Q0[0:67, h, 0:n] if c == 0 else Q1[:, h, 0:n], start=True, stop=False)
                b.mm(ps[0:32, o:o + n], IDB[0:32, 0:32], DTABS[:, h, :], start=False, stop=True)
            b.act(e[0:32, 0:64], ps[0:32, 0:64], ACTF.Exp, bias=SLOPES[h] * 128.0 * 16)
            for c in range(2):
                o = c * 32
                o0 = (h * 2 + c) * 32
                b.mm(POD[:, o0:o0 + n], VSN[0:32, h * 128:(h + 1) * 128], e[0:32, o:o + n], start=False, stop=True, skip=True)
                b.mm(POD[:, 256 + o0:256 + o0 + n], ONES[0:32, :], e[0:32, o:o + n], start=False, stop=True, skip=True)
        for h in range(4):
            o0 = (h * 2) * 32
            finish_head(h, n, POD[:, o0:o0 + n], POD[:, o0 + 32:o0 + 32 + n],
                        POD[:, 256 + o0:256 + o0 + n], POD[:, 256 + o0 + 32:256 + o0 + 32 + n], extra=[POD[:, :]])

    b3 = [0]

    def bank3():
        b3[0] += 1
        return PS[b3[0] % 4]

    order = [0, 1, 2, 3, 4]
    XNTL = [XNT, XNTB]
    c00, n00 = TILES[order[0]]
    norm(lambda c: XT[:, c, c00:c00 + n00], lambda c: XNTL[0][:, c, :n00], n00, 8, OND1)
    for idx, t in enumerate(order):
        c0, n = TILES[t]
        samp = n != 512
        xn = XNTL[idx % 2]
        S.dma("sp", Q0[64:67, :, 0:n], dap(qaug_d, c0, [(4 * NT, 3), (NT, 4), (1, n)]))
        S.dma("sp", Q1[0:3, :, 0:n], dap(qaug_d, c0, [(4 * NT, 3), (NT, 4), (1, n)]))

        def ev1(oc, ps, n=n):
            if oc < 4:
                b.act(Q0[0:64, oc, :n], ps[0:64, :n], ACTF.Copy, scale=0.125)
                b.act(Q1[64:128, oc, :n], ps[64:128, :n], ACTF.Copy, scale=0.125)
            else:
                b.evac(QM[:, oc - 4, :n], ps[:, :n])
        proj(WSQ[1], range(8), lambda kc: xn[:, kc, :n], n, ev1)
        mem_attend(n, MKs if samp else MKpL[1], MVs if samp else MVpL[1])
        if samp:
            diff_attend_sample()
        else:
            diff_attend_prompt(t)
        mid1 = None
        if idx + 1 < len(order):
            def mid1(idx=idx):
                c1, n1 = TILES[order[idx + 1]]
                xn1 = XNTL[(idx + 1) % 2]
                norm(lambda c: XT[:, c, c1:c1 + n1], lambda c: xn1[:, c, :n1], n1, 8, OND1)
        wout_residual(WSQ[0], c0, n, mid=mid1)

    if STAGE < 5:
        return finish()
    for t, (c0, n) in enumerate(TILES):
        norm(lambda c: XT[:, c, c0:c0 + n], lambda c: XN[:, c, c0:c0 + n], n, 40, OND1)
    mlp(1)
    if STAGE < 6:
        return finish()

    so = [0]
    YNL = [YN, S.sbuf("YNB", [128, 8, 512], F32, at=RW + 16384)]
    c00, n00 = TILES[0]
    norm(lambda c: XT[:, c, c00:c00 + n00], lambda c: YNL[0][:, c, :n00], n00, 56, OND1)
    for t, (c0, n) in enumerate(TILES):
        YN = YNL[t % 2]
        if t + 1 < len(TILES):
            c1, n1 = TILES[t + 1]
            yn1 = YNL[(t + 1) % 2]
            norm(lambda c: XT[:, c, c1:c1 + n1], lambda c: yn1[:, c, :n1], n1, 56, OND1)
        nb = 4 if n == 512 else 1
        for blk in range(nb):
            rows = 128 if n == 512 else 32
            stg = STG[so[0] % 2]
            so[0] += 1
            for half in range(2):
                ps = bank()
                for cc in range(4):
                    c = half * 4 + cc
                    b.tr(ps[0:rows, cc * 128:(cc + 1) * 128], YN[:, c, blk * 128:blk * 128 + rows], IDF[:])
                b.evac(stg[0:rows, half * 512:(half + 1) * 512], ps[0:rows, :])
            S.dma("sp", dap(y_d, (c0 + blk * 128) * D, [(D, rows), (1, D)]), stg[0:rows, :])

    return finish()


_PROG = {}
_STAGE = [99]


def _host_tables(j):
    pos = np.zeros(NT, np.int64)
    for i in range(16):
        gb = 4 * i + j
        pos[i * 128:(i + 1) * 128] = gb * 128 + np.arange(128)
    pos[NP:] = PAST + np.arange(NSM)
    qa = pos // 128
    qb = pos % 128
    qaug = np.zeros((3, 4, NT), np.float32)
    for h in range(4):
        qaug[0, h] = -SLOPES[h] * 128.0 * qa
        qaug[1, h] = -SLOPES[h] * qb
        qaug[2, h] = SLOPES[h]
    kaug = np.zeros((3, 512), np.float32)
    kaug[0] = 1.0
    kaug[1] = 1.0
    kaug[2] = np.arange(512) % 128
    ii = np.arange(128)[:, None]
    qq = np.arange(128)[None, :]
    dtab = np.zeros((128, 4, 4, 128), np.float32)
    for h in range(4):
        dd = np.where(ii // 64 > qq // 64, NEG, np.where(ii > qq, -2.0 * SLOPES[h] * (ii - qq), 0.0))
        for r in range(4):
            if r < j:
                dtab[:, h, r, :] = 0.0
            elif r == j:
                dtab[:, h, r, :] = dd
            else:
                dtab[:, h, r, :] = NEG
    i2 = np.arange(32)[:, None]
    q2 = np.arange(32)[None, :]
    dtabs = np.zeros((32, 4, 32), np.float32)
    for h in range(4):
        dtabs[:, h, :] = np.where(i2 > q2, -2.0 * SLOPES[h] * (i2 - q2), 0.0)
    rc16 = np.zeros((4, 16), np.float32)
    for g, w in enumerate((2, 4, 8, 16)):
        if j == 0:
            rc16[g] = 1.0 / np.minimum(np.arange(16) + 1, w)
        else:
            rc16[g] = 1.0 / w
    import ml_dtypes
    qaug = qaug.astype(ml_dtypes.bfloat16)
    kaug = kaug.astype(ml_dtypes.bfloat16)
    return dict(qaug=qaug, kaug=kaug, dtab=dtab.reshape(128, -1), dtabs=dtabs.reshape(32, -1),
                rc16=rc16.reshape(1, 64), identd=np.eye(128, dtype=np.float32))


def kernel(x_prompt, x_sample, mem_prompt, cache_k, cache_v, cache_mem_k, cache_mem_v, state_pool,
           g_attn, w_in, w_out, g_mem, w_mem_kv, g_ffn, w_ff1, w_ff2, w_pool, pool_scale,
           lambda_qk, g_subln, g_kv, w_kv, g_final):
    f = lambda a: np.ascontiguousarray(np.asarray(a, dtype=np.float32))
    x_prompt, x_sample, mem_prompt = f(x_prompt), f(x_sample), f(mem_prompt)
    cache_k, cache_v, cache_mem_k, cache_mem_v = f(cache_k), f(cache_v), f(cache_mem_k), f(cache_mem_v)
    state_pool = f(state_pool)
    if "p" not in _PROG:
        _PROG["p"] = build_program(_STAGE[0])
    nc = _PROG["p"][0]

    def colv(v):
        return f(v).reshape(8, 128).T
    cols = np.concatenate([colv(g_attn[0]), colv(g_attn[1]), colv(g_mem[0]), colv(g_mem[1]),
                           colv(g_ffn[0]), colv(g_ffn[1]), colv(g_kv), colv(g_final),
                           f(pool_scale).reshape(4, 128).T, f(g_subln).reshape(1, 128).T], axis=1)
    shared = dict(w_in=f(w_in), w_out=f(w_out), w_mem_kv=f(w_mem_kv), w_ff1=f(w_ff1), w_ff2=f(w_ff2),
                  w_pool=f(w_pool).reshape(4, 128, 128), w_kv=f(w_kv), lq=f(lambda_qk).reshape(1, 256),
                  cols=np.ascontiguousarray(cols.astype(np.float32)))
    in_maps = []
    for c in range(8):
        bq, j = c // 4, c % 4
        xp = x_prompt[bq].reshape(64, 128, D)
        blocks = xp[j::4]
        xx = np.concatenate([blocks.reshape(NP, D), x_sample[c]], axis=0)
        xh = np.zeros((16, 16, D), np.float32)
        for i in range(16):
            gb = 4 * i + j
            if gb > 0:
                xh[i] = x_prompt[bq, gb * 128 - 16:gb * 128]
        m = dict(shared)
        m.update(x=np.ascontiguousarray(xx), xh=xh.reshape(NH, D), mem=mem_prompt[bq],
                 ck=cache_k[c].reshape(PAST, 512), cv=cache_v[c].reshape(PAST, 512),
                 cmk=np.ascontiguousarray(cache_mem_k[:, c].reshape(2, 256, 512)),
                 cmv=np.ascontiguousarray(cache_mem_v[:, c].reshape(2, 256, 512)),
                 spool=state_pool[0, c])
        m.update(_host_tables(j))
        in_maps.append(m)
    res = run_bass_kernel_spmd(nc, in_maps, core_ids=list(range(8)))
    R = res.results
    y_prompt = np.zeros((2, SEQ, D), np.float32)
    k_prompt = np.zeros((2, SEQ, 4, 128), np.float32)
    v_prompt = np.zeros((2, SEQ, 4, 128), np.float32)
    y_sample = np.zeros((8, NSM, D), np.float32)
    k_sample = np.zeros((8, NSM, 4, 128), np.float32)
    v_sample = np.zeros((8, NSM, 4, 128), np.float32)
    pool_sample = np.zeros((1, 8, 15, 512), np.float32)
    for c in range(8):
        bq, j = c // 4, c % 4
        r = R[c]
        yv = y_prompt[bq].reshape(64, 128, D)
        kv = k_prompt[bq].reshape(64, 128, 512)
        vv = v_prompt[bq].reshape(64, 128, 512)
        yv[j::4] = r["y"][:NP].reshape(16, 128, D)
        kv[j::4] = r["ko"][:NP].reshape(16, 128, 512)
        vv[j::4] = r["vo"][:NP].reshape(16, 128, 512)
        y_sample[c] = r["y"][NP:]
        k_sample[c] = r["ko"][NP:].reshape(NSM, 4, 128)
        v_sample[c] = r["vo"][NP:].reshape(NSM, 4, 128)
        pool_sample[0, c] = r["poolo"][1]
    mem_k_prompt = np.stack([R[0]["memk"], R[4]["memk"]], axis=1).reshape(2, 2, 256, 4, 128)
    mem_v_prompt = np.stack([R[0]["memv"], R[4]["memv"]], axis=1).reshape(2, 2, 256, 4, 128)
    pool_prompt = np.stack([R[3]["poolo"][0], R[7]["poolo"][0]], axis=0)[None]
    return (y_prompt, y_sample, np.ascontiguousarray(mem_k_prompt), np.ascontiguousarray(mem_v_prompt),
            np.ascontiguousarray(pool_prompt), k_prompt, v_prompt, pool_sample, k_sample, v_sample)
```

```python
import math
import contextlib
import numpy as np
import concourse.bass as bass
import concourse.mybir as mybir
from concourse.bass_utils import run_bass_kernel_spmd

DT = mybir.dt
F32 = DT.float32
BF16 = DT.bfloat16
ALU = mybir.AluOpType
ACTF = mybir.ActivationFunctionType
ENGS = ("pe", "act", "dve", "pool", "sp")

D = 1024
NP = 2048
NSM = 32
NT = NP + NSM
NH = 256
SEQ = 8192
PAST = 2048
EPS = 1e-6
LAM_I = 0.8 - 0.6 * math.exp(-0.3 * 1)
SLOPES = [2.0 ** (-8.0 * (h + 1) / 4) for h in range(4)]
NEG = -30000.0
TILES = [(0, 512), (512, 512), (1024, 512), (1536, 512), (2048, 32)]
NKBUF = 4


def _prod(xs):
    r = 1
    for x in xs:
        r *= int(x)
    return r


class Op:
    __slots__ = ("idx", "eng", "fn", "kind", "deps", "flag", "mile", "dsem", "dval", "dprev")

    def __init__(self, idx, eng, fn, kind):
        self.idx = idx
        self.eng = eng
        self.fn = fn
        self.kind = kind
        self.deps = set()
        self.flag = False
        self.mile = 0
        self.dsem = None
        self.dval = 0
        self.dprev = 0


class Sched:
    def __init__(self, nc):
        self.nc = nc
        self.ops = []
        self.sb_top = 16512
        self.sb_limit = 229344
        self.tinfo = {}
        self.wr = {}
        self.rd = {}
        self.n_dma_sems = {"sp": 48, "pool": 32, "act": 8}

    def sbuf(self, name, shape, dtype, at=None):
        dsize = DT.size(dtype)
        per_part = _prod(shape[1:]) * dsize
        if at is None:
            at = (self.sb_top + 63) // 64 * 64
            self.sb_top = at + per_part
        assert at + per_part <= self.sb_limit, (name, at, per_part)
        t = self.nc.alloc_sbuf_tensor_at(name, list(shape), dtype, offset=int(at))
        self.tinfo[t.name] = ("SB", int(at), _prod(shape[1:]), dsize)
        return t

    def reserve(self, nbytes):
        at = (self.sb_top + 63) // 64 * 64
        self.sb_top = at + nbytes
        assert self.sb_top <= self.sb_limit, ("reserve", at, nbytes)
        return at

    def psum(self, name, shape=(128, 512), dtype=F32):
        t = self.nc.alloc_psum_tensor(name, list(shape), dtype)
        self.tinfo[t.name] = ("PS:" + t.name, 0, _prod(shape[1:]), DT.size(dtype))
        return t

    def dram(self, name, shape, dtype, kind="Internal"):
        t = self.nc.dram_tensor(name, list(shape), dtype, kind=kind)
        self.tinfo[t.name] = ("DR:" + t.name, 0, None, DT.size(dtype))
        return t

    def regions(self, ap):
        t = ap.tensor
        key, base, per_part, dsize = self.tinfo[t.name]
        off = int(ap.offset)
        dims = [(int(s), int(c)) for s, c in ap.ap]
        if per_part is None:
            lo = hi = off
            for s, c in dims:
                if c > 1:
                    if s >= 0:
                        hi += s * (c - 1)
                    else:
                        lo += s * (c - 1)
            return [(key, 0, 1, lo, hi + 1)]
        p0 = off // per_part
        inoff = off % per_part
        pstride, pcount = dims[0]
        if pstride == 0 or pcount == 1:
            p1 = p0 + 1
        else:
            assert pstride == per_part, (t.name, dims, per_part)
            p1 = p0 + pcount
        lo = hi = inoff
        for s, c in dims[1:]:
            if c > 1:
                if s >= 0:
                    hi += s * (c - 1)
                else:
                    lo += s * (c - 1)
        if key.startswith("PS:"):
            b0 = (lo * dsize) // 2048
            b1 = (hi * dsize) // 2048
            return [(key + ":%d" % bk, p0 // 32 * 32, (p1 + 31) // 32 * 32, 0, 2048) for bk in range(b0, b1 + 1)]
        return [(key, p0, p1, base + lo * dsize, base + (hi + 1) * dsize)]

    def _add(self, eng, fn, reads, writes, kind="c"):
        op = Op(len(self.ops), eng, fn, kind)
        rregs = [rg for a in reads if a is not None for rg in self.regions(a)]
        wregs = [rg for a in writes if a is not None for rg in self.regions(a)]
        deps = op.deps
        for (key, p0, p1, lo, hi) in rregs:
            for w in self.wr.get(key, ()):
                if w[0] < p1 and p0 < w[1] and w[2] < hi and lo < w[3]:
                    deps.add(w[4])
        for (key, p0, p1, lo, hi) in wregs:
            for w in self.wr.get(key, ()):
                if w[0] < p1 and p0 < w[1] and w[2] < hi and lo < w[3]:
                    deps.add(w[4])
            for r in self.rd.get(key, ()):
                if r[0] < p1 and p0 < r[1] and r[2] < hi and lo < r[3]:
                    deps.add(r[4])
        for (key, p0, p1, lo, hi) in wregs:
            wl = self.wr.setdefault(key, [])
            wl[:] = [w for w in wl if not (p0 <= w[0] and w[1] <= p1 and lo <= w[2] and w[3] <= hi)]
            wl.append([p0, p1, lo, hi, op.idx])
            rl = self.rd.get(key)
            if rl:
                rl[:] = [r for r in rl if not (p0 <= r[0] and r[1] <= p1 and lo <= r[2] and r[3] <= hi)]
        inorder = kind == "c"
        for (key, p0, p1, lo, hi) in rregs:
            rl = self.rd.setdefault(key, [])
            done = False
            if inorder:
                for r in rl:
                    if r[5] == eng and r[0] == p0 and r[1] == p1 and r[2] == lo and r[3] == hi:
                        r[4] = op.idx
                        done = True
                        break
            if not done:
                rl.append([p0, p1, lo, hi, op.idx, eng if inorder else None])
        deps.discard(op.idx)
        self.ops.append(op)
        return op

    def c(self, eng, fn, reads, writes):
        return self._add(eng, fn, reads, writes, "c")

    def dma(self, q, out, in_, **kw):
        def fn(e, out=out, in_=in_, kw=kw):
            return e.dma_start(out=out, in_=in_, **kw)
        return self._add(q, fn, [in_], [out], "d")

    def collective(self, q, fn, reads, writes):
        return self._add(q, fn, reads, writes, "cc")

    def emit(self):
        nc = self.nc
        ops = self.ops
        for op in ops:
            for d in op.deps:
                p = ops[d]
                if p.kind == "c":
                    if p.eng == "pe" and op.eng == "pe" and op.kind == "c":
                        continue
                    p.flag = True
        cnt = {e: 0 for e in ENGS}
        for op in ops:
            if op.kind == "c" and op.flag:
                cnt[op.eng] += 1
                op.mile = cnt[op.eng]
        qcount = {}
        ncc = 0
        for op in ops:
            if op.kind == "d":
                k = qcount.get(op.eng, 0)
                qcount[op.eng] = k + 1
                ns = self.n_dma_sems[op.eng]
                op.dsem = (op.eng, k % ns)
                op.dval = 16 * (k // ns + 1)
                op.dprev = 16 * (k // ns)
            elif op.kind == "cc":
                op.dsem = ("cc", ncc)
                ncc += 1
                op.dval = 1
                op.dprev = 0
        with contextlib.ExitStack() as st:
            esem = {e: st.enter_context(nc.semaphore("s_" + e)) for e in ENGS}
            dsem = {}
            for q, n in self.n_dma_sems.items():
                for i in range(min(n, qcount.get(q, 0))):
                    dsem[(q, i)] = st.enter_context(nc.semaphore("d_%s_%d" % (q, i)))
            for i in range(ncc):
                dsem[("cc", i)] = st.enter_context(nc.semaphore("s_cc%d" % i))
            block = st.enter_context(nc.Block())
            per_eng = {e: [op for op in ops if op.eng == e] for e in ENGS}
            final = {}
            for op in ops:
                if op.kind in ("d", "cc"):
                    final[op.dsem] = max(final.get(op.dsem, 0), op.dval)

            def run(engname, eobj):
                waited = {}
                for op in per_eng[engname]:
                    waits = {}
                    for d in op.deps:
                        p = ops[d]
                        if p.kind == "c":
                            if p.eng == "pe" and engname == "pe" and op.kind == "c":
                                continue
                            s, v = ("e", p.eng), p.mile
                        else:
                            s, v = ("d", p.dsem), p.dval
                        if v > waits.get(s, 0):
                            waits[s] = v
                    if op.kind in ("d", "cc") and op.dprev > 0:
                        s = ("d", op.dsem)
                        if op.dprev > waits.get(s, 0):
                            waits[s] = op.dprev
                    for s, v in waits.items():
                        if waited.get(s, 0) >= v:
                            continue
                        waited[s] = v
                        sem = esem[s[1]] if s[0] == "e" else dsem[s[1]]
                        eobj.wait_ge(sem, v)
                    ins = op.fn(eobj)
                    if op.kind == "c":
                        if op.flag:
                            ins.then_inc(esem[op.eng], 1)
                    elif op.kind == "d":
                        ins.then_inc(dsem[op.dsem], 16)
                    else:
                        ins.then_inc(dsem[op.dsem], 1)
                if engname == "sp":
                    for s, v in final.items():
                        eobj.wait_ge(dsem[s], v)

            @block.tensor
            def _(e):
                run("pe", e)

            @block.scalar
            def _(e):
                run("act", e)

            @block.vector
            def _(e):
                run("dve", e)

            @block.gpsimd
            def _(e):
                run("pool", e)

            @block.sync
            def _(e):
                run("sp", e)
        return cnt, qcount


class B:
    def __init__(self, S):
        self.S = S
        self.evq = 0

    def mm(self, out, lhsT, rhs, start=True, stop=True, skip=False):
        if skip:
            self.S.c("pe", lambda e: e.matmul(out, lhsT, rhs, start=start, stop=stop, skip_group_check=True),
                     [lhsT, rhs], [out])
        else:
            self.S.c("pe", lambda e: e.matmul(out, lhsT, rhs, start=start, stop=stop), [lhsT, rhs], [out])

    def tr(self, out, in_, ident):
        self.S.c("pe", lambda e: e.transpose(out, in_, ident), [in_, ident], [out])

    def act(self, out, in_, func, scale=None, bias=None, extra=()):
        kw = {}
        rd = [in_] + list(extra)
        if scale is not None:
            kw["scale"] = scale
            if not isinstance(scale, (int, float)):
                rd.append(scale)
        if bias is not None:
            kw["bias"] = bias
            if not isinstance(bias, (int, float)):
                rd.append(bias)
        self.S.c("act", lambda e: e.activation(out, in_, func, **kw), rd, [out])

    def tt(self, out, a, b, op, eng="dve"):
        self.S.c(eng, lambda e: e.tensor_tensor(out, a, b, op), [a, b], [out])

    def stt(self, out, in0, scalar, in1, op0, op1, eng="dve"):
        rd = [in0, in1]
        if not isinstance(scalar, (int, float)):
            rd.append(scalar)
        self.S.c(eng, lambda e: e.scalar_tensor_tensor(out, in0, scalar, in1, op0, op1), rd, [out])

    def ts(self, out, in0, s1, s2, op0, op1=None, eng="dve"):
        rd = [in0]
        for s in (s1, s2):
            if s is not None and not isinstance(s, (int, float)):
                rd.append(s)
        if op1 is None:
            self.S.c(eng, lambda e: e.tensor_scalar(out, in0, s1, None, op0), rd, [out])
        else:
            self.S.c(eng, lambda e: e.tensor_scalar(out, in0, s1, s2, op0, op1), rd, [out])

    def copy(self, out, in_, eng="dve"):
        if eng == "act":
            self.S.c("act", lambda e: e.copy(out, in_), [in_], [out])
        else:
            self.S.c(eng, lambda e: e.tensor_copy(out, in_), [in_], [out])

    def evac(self, out, in_):
        self.evq += 1
        self.copy(out, in_, eng="act" if self.evq % 2 else "dve")

    def recip(self, out, in_, extra=()):
        self.S.c("dve", lambda e: e.reciprocal(out, in_), [in_] + list(extra), [out])

    def memset(self, ap, val, eng="dve"):
        self.S.c(eng, lambda e: e.memset(ap, val), [], [ap])

    def reduce_sum(self, out, in_):
        self.S.c("dve", lambda e: e.reduce_sum(out, in_, axis=mybir.AxisListType.X), [in_], [out])


def dap(t, offset, dims):
    return bass.AP(t, int(offset), [[int(s), int(c)] for s, c in dims])


def build_program(STAGE=99):
    nc = bass.Bass("TRN2", target_bir_lowering=False)
    S = Sched(nc)
    b = B(S)

    def finish():
        stats = S.emit()
        return nc, stats, len(S.ops)
    EI, EO = "ExternalInput", "ExternalOutput"
    x_d = S.dram("x", [NT, D], F32, EI)
    xh_d = S.dram("xh", [NH, D], F32, EI)
    mem_d = S.dram("mem", [256, D], F32, EI)
    ck_d = S.dram("ck", [PAST, 512], F32, EI)
    cv_d = S.dram("cv", [PAST, 512], F32, EI)
    cmk_d = S.dram("cmk", [2, 256, 512], F32, EI)
    cmv_d = S.dram("cmv", [2, 256, 512], F32, EI)
    spool_d = S.dram("spool", [15, 512], F32, EI)
    w_in_d = S.dram("w_in", [2, D, D], F32, EI)
    w_out_d = S.dram("w_out", [2, D, D], F32, EI)
    w_mem_d = S.dram("w_mem_kv", [2, D, D], F32, EI)
    w_ff1_d = S.dram("w_ff1", [2, D, 4 * D], F32, EI)
    w_ff2_d = S.dram("w_ff2", [2, 4 * D, D], F32, EI)
    w_pool_d = S.dram("w_pool", [4, 128, 128], F32, EI)
    w_kv_d = S.dram("w_kv", [D, D], F32, EI)
    lq_d = S.dram("lq", [1, 256], F32, EI)
    cols_d = S.dram("cols", [128, 69], F32, EI)
    qaug_d = S.dram("qaug", [3, 4, NT], BF16, EI)
    kaug_d = S.dram("kaug", [3, 512], BF16, EI)
    dtab_d = S.dram("dtab", [128, 4 * 4 * 128], F32, EI)
    dtabs_d = S.dram("dtabs", [32, 4 * 32], F32, EI)
    rc16_d = S.dram("rc16", [1, 64], F32, EI)
    ident_d = S.dram("identd", [128, 128], F32, EI)

    y_d = S.dram("y", [NT, D], F32, EO)
    memk_d = S.dram("memk", [2, 256, 512], F32, EO)
    memv_d = S.dram("memv", [2, 256, 512], F32, EO)
    poolo_d = S.dram("poolo", [2, 15, 512], F32, EO)
    ko_d = S.dram("ko", [NT, 512], F32, EO)
    vo_d = S.dram("vo", [NT, 512], F32, EO)

    xk_in = [S.dram("xk_in%d" % i, [512, 1024], BF16) for i in range(2)]
    xk_out = [S.dram("xk_out%d" % i, [2048, 1024], BF16) for i in range(2)]
    xv_in = [S.dram("xv_in%d" % i, [256, 2048], BF16) for i in range(2)]
    xv_out = [S.dram("xv_out%d" % i, [1024, 2048], BF16) for i in range(2)]

    XT = S.sbuf("XT", [128, 8, NT], F32)
    RW = S.reserve(46080)
    WSQ = [S.sbuf("WSQ%d" % i, [128, 8, 1024], BF16) for i in range(2)]
    WM = S.reserve(32768)
    MKpL = [S.sbuf("MKp%d" % i, [128, 4, 256], BF16) for i in range(2)]
    MVpL = [S.sbuf("MVp%d" % i, [128, 2, 512], BF16) for i in range(2)]
    MKs = S.sbuf("MKs", [128, 4, 256], BF16)
    MVs = S.sbuf("MVs", [128, 2, 512], BF16)
    DTAB = S.sbuf("DTAB", [128, 4, 4, 128], BF16)
    DTABS = S.sbuf("DTABS", [32, 4, 32], BF16)
    TF = [S.sbuf("TF%d" % i, [128, 512], F32) for i in range(4)]
    SQ = [S.sbuf("SQ%d" % i, [128, 512], BF16) for i in range(3)]
    IDF = S.sbuf("IDF", [128, 128], F32)
    IDB = S.sbuf("IDB", [128, 128], BF16)
    ONES = S.sbuf("ONES", [128, 128], BF16)
    OND1 = S.sbuf("OND1", [128, 128], BF16)
    OND2 = S.sbuf("OND2", [128, 128], BF16)
    COLS = S.sbuf("COLS", [128, 72], F32)
    LQ = S.sbuf("LQ", [128, 256], F32)
    LT = S.sbuf("LT", [128, 64], F32)
    LS = S.sbuf("LS", [128, 8], F32)
    RC16 = S.sbuf("RC16", [128, 4, 16], F32)
    WPOOL = S.sbuf("WPOOL", [128, 4, 128], BF16)
    KSN0 = S.sbuf("KSN0", [128, 4, 32], BF16)
    KSN1 = S.sbuf("KSN1", [128, 4, 32], BF16)
    VSN = S.sbuf("VSN", [32, 512], BF16)
    SMALL = S.sbuf("SMALL", [128, 4, 16], F32)

    XNT = S.sbuf("XNT", [128, 8, 512], BF16, at=RW + 0)
    TM = S.sbuf("TM", [128, 8, 512], BF16, at=RW + 0)
    U = S.sbuf("U", [128, 4, 4, 144], F32, at=RW + 8192)
    UH = S.sbuf("UH", [128, 4, 16, 16], F32, at=RW + 17408)
    PA = S.sbuf("PA", [128, 4, 4, 144], F32, at=RW + 21504)
    PB = S.sbuf("PB", [128, 3, 4, 144], F32, at=RW + 30720)
    DD = S.sbuf("DD", [128, 4, 512], BF16, at=RW + 37632)
    QM = S.sbuf("QM", [128, 4, 512], BF16, at=RW + 41984)
    TMm1 = S.sbuf("TMm1", [128, 4, 512], BF16, at=RW + 37888)
    TMt1 = S.sbuf("TMt1", [128, 4, 512], BF16, at=WM + 28672)
    XNTB = S.sbuf("XNTB", [128, 8, 512], BF16, at=RW + 20480)
    Q0 = S.sbuf("Q0", [128, 4, 512], BF16, at=RW + 8192)
    Q1 = S.sbuf("Q1", [128, 4, 512], BF16, at=RW + 12288)
    OA = S.sbuf("OA", [128, 512], F32, at=RW + 16384)
    OB = S.sbuf("OB", [128, 512], F32, at=RW + 18432)
    XN = S.sbuf("XN", [128, 8, NT], BF16, at=RW + 0)
    HQ = [S.sbuf("HQ%d" % i, [128, 4, 512], BF16, at=RW + 33280 + 4096 * i) for i in range(2)]
    RL = [S.sbuf("RL%d" % i, [128, 512], F32, at=RW + 41472 + 2048 * i) for i in range(2)]
    XHT = S.sbuf("XHT", [128, 8, 256], F32, at=RW + 21504)
    XNH = S.sbuf("XNH", [128, 8, 256], BF16, at=RW + 29696)
    MEMT = S.sbuf("MEMT", [128, 8, 256], F32, at=RW + 0)
    MEMN = S.sbuf("MEMN", [128, 8, 256], BF16, at=RW + 8192)
    YN = S.sbuf("YN", [128, 8, 512], F32, at=RW + 0)
    WM1 = [S.sbuf("WM1_%d" % i, [128, 8, 512], BF16, at=WM + 8192 * i) for i in range(2)]
    WM2 = [S.sbuf("WM2_%d" % i, [128, 4, 1024], BF16, at=WM + 16384 + 8192 * i) for i in range(2)]
    KB0 = [S.sbuf("KB0_%d" % i, [128, 512], BF16, at=WM + 1024 * i) for i in range(NKBUF)]
    KB1 = [S.sbuf("KB1_%d" % i, [128, 512], BF16, at=WM + 4096 + 1024 * i) for i in range(NKBUF)]
    VB = [S.sbuf("VB_%d" % i, [128, 4, 128], BF16, at=WM + 8192 + 1024 * i) for i in range(NKBUF)]
    EB = [S.sbuf("EB_%d" % i, [128, 512], BF16, at=WM + 12288 + 1024 * i) for i in range(8)]
    CKS = S.sbuf("CKS", [128, 4, 512], F32, at=WM + 20480)
    CVSF = S.sbuf("CVSF", [128, 4, 512], F32, at=RW + 21504)
    NEB = 6
    EBP = [S.sbuf("EBP%d" % i, [128, 2, 512], BF16, at=WM + 12288 + 2048 * i) for i in range(NEB)]
    ESP = [S.sbuf("ESP%d" % i, [128, 2, 512], BF16, at=WM + 24576 + 2048 * i) for i in range(2)]
    ACC = S.sbuf("ACC", [128, 2, 512], F32, at=RW + 29696)
    HI = S.sbuf("HI", [128, 2, 512], BF16, at=RW + 33792)
    LO = S.sbuf("LO", [128, 2, 512], BF16, at=RW + 35840)
    XIN = S.sbuf("XIN", [128, 4, 1024], F32, at=WM + 0)
    XIN2 = S.sbuf("XIN2", [128, 4, 1024], F32, at=WM + 16384)
    STG = [S.sbuf("STG%d" % i, [128, 1024], F32, at=WM + 16384 + 4096 * i) for i in range(2)]
    VST = [S.sbuf("VST%d" % i, [128, 512], BF16, at=WM + 24576 + 1024 * i) for i in range(2)]
    KST = [S.sbuf("KST%d" % i, [128, 512], BF16, at=WM + 26624 + 1024 * i) for i in range(2)]
    MST = S.sbuf("MST", [128, 2, 1024], F32, at=WM + 0)
    CMS = S.sbuf("CMS", [128, 2, 512], F32, at=WM + 8192)

    PP = [S.psum("PP%d" % i, (128, 1024)) for i in range(4)]
    PS = [PP[i // 2][:, (i % 2) * 512:(i % 2) * 512 + 512] for i in range(8)]
    psr = [0]

    def bank():
        psr[0] += 1
        return PS[psr[0] % 8]

    def gcol(i):
        return COLS[:, i:i + 1]

    S.dma("sp", IDF[:], ident_d[:, :])
    S.dma("sp", COLS[:, 0:69], cols_d[:, :])
    S.dma("sp", LQ[:], dap(lq_d, 0, [(0, 128), (1, 256)]))
    S.dma("sp", RC16[:], dap(rc16_d, 0, [(0, 128), (16, 4), (1, 16)]))
    def load_sq(dst, src_t, off):
        for kc in range(8):
            S.dma("pool", dst[:, kc, :], dap(src_t, off + kc * 128 * 1024, [(1024, 128), (1, 1024)]))

    load_sq(WSQ[0], w_in_d, 0)
    load_sq(WSQ[1], w_mem_d, 0)
    S.dma("pool", WPOOL[:], dap(w_pool_d, 0, [(128, 128), (16384, 4), (1, 128)]))
    S.dma("pool", DTAB[:], dap(dtab_d, 0, [(2048, 128), (512, 4), (128, 4), (1, 128)]))
    S.dma("pool", DTABS[:], dap(dtabs_d, 0, [(128, 32), (32, 4), (1, 32)]))
    b.copy(IDB[:], IDF[:])
    b.memset(ONES[:], 1.0)
    b.memset(OND1[:], 1.0 / 1024)
    b.memset(OND2[:], 1.0 / 128)
    b.tt(LT[:], LQ[:, 0:64], LQ[:, 64:128], ALU.mult)
    b.reduce_sum(LS[:, 0:1], LT[:])
    b.tt(LT[:], LQ[:, 128:192], LQ[:, 192:256], ALU.mult)
    b.reduce_sum(LS[:, 1:2], LT[:])
    b.act(LS[:, 4:6], LS[:, 0:2], ACTF.Exp)
    b.tt(LS[:, 6:7], LS[:, 5:6], LS[:, 4:5], ALU.subtract)
    b.ts(LS[:, 2:3], LS[:, 6:7], -LAM_I, None, ALU.add)
    b.ts(LS[:, 3:4], gcol(68), 1.0 - LAM_I, None, ALU.mult)
    NEGLAM = LS[:, 2:3]
    GSUB = LS[:, 3:4]

    sqr = [0]

    def norm(src, dst, n, g0, invn_ones, nchunks=8):
        ps = bank()
        for c in range(nchunks):
            sq = SQ[sqr[0] % 3]
            sqr[0] += 1
            b.act(sq[:, :n], src(c), ACTF.Square)
            b.mm(ps[:, :n], invn_ones[:], sq[:, :n], start=(c == 0), stop=(c == nchunks - 1))
        b.act(TF[0][:, :n], ps[:, :n], ACTF.Ln, bias=EPS)
        b.act(TF[1][:, :n], TF[0][:, :n], ACTF.Exp, scale=-0.5)
        for c in range(nchunks):
            b.stt(dst(c), src(c), gcol(g0 + c), TF[1][:, :n], ALU.mult, ALU.mult)

    XS = S.sbuf("XS", [128, 2, 1024], F32, at=RW + 0)

    def xload(t1):
        c1, n1 = TILES[t1]
        if n1 == 512:
            for hf in range(2):
                S.dma("sp", XS[:], dap(x_d, (c1 + 256 * hf) * D, [(D, 128), (128 * D, 2), (1, D)]))
                for c in range(8):
                    ps = bank()
                    for blk in range(2):
                        b.tr(ps[:, blk * 128:(blk + 1) * 128], XS[:, blk, c * 128:(c + 1) * 128], IDF[:])
                    b.evac(XT[:, c, c1 + 256 * hf:c1 + 256 * hf + 256], ps[:, 0:256])
        else:
            S.dma("sp", XS[0:32, 0, :], dap(x_d, c1 * D, [(D, 32), (1, D)]))
            ps = bank()
            for c in range(8):
                b.tr(ps[:, c * 32:(c + 1) * 32], XS[0:32, 0, c * 128:(c + 1) * 128], IDF[0:32, 0:32])
            b.evac(XT[:, :, c1:c1 + 32], ps[:, 0:256].rearrange("p (c t) -> p c t", c=8))

    for t, (c0, n) in enumerate(TILES[:1]):
        if n == 512:
            xin = XIN if t % 2 == 0 else XIN2
            S.dma("sp", xin[:], dap(x_d, c0 * D, [(D, 128), (128 * D, 4), (1, D)]))
            for c in range(8):
                ps = bank()
                for blk in range(4):
                    b.tr(ps[:, blk * 128:(blk + 1) * 128], xin[:, blk, c * 128:(c + 1) * 128], IDF[:])
                b.evac(XT[:, c, c0:c0 + 512], ps[:, :])
        else:
            S.dma("sp", XIN[0:32, 0, :], dap(x_d, c0 * D, [(D, 32), (1, D)]))
            ps = bank()
            for c in range(8):
                b.tr(ps[:, c * 32:(c + 1) * 32], XIN[0:32, 0, c * 128:(c + 1) * 128], IDF[0:32, 0:32])
            b.evac(XT[:, :, c0:c0 + 32], ps[:, 0:256].rearrange("p (c t) -> p c t", c=8))
    S.dma("sp", XIN2[:, 0:2, :], dap(xh_d, 0, [(D, 128), (128 * D, 2), (1, D)]))
    for c in range(8):
        ps = bank()
        for blk in range(2):
            b.tr(ps[:, blk * 128:(blk + 1) * 128], XIN2[:, blk, c * 128:(c + 1) * 128], IDF[:])
        b.evac(XHT[:, c, :], ps[:, 0:256])

    def mem_kv_prompt(l, first, W):
        if first:
            S.dma("sp", MST[:], dap(mem_d, 0, [(D, 128), (128 * D, 2), (1, D)]))
            for c in range(8):
                ps = bank()
                for mb in range(2):
                    b.tr(ps[:, mb * 128:(mb + 1) * 128], MST[:, mb, c * 128:(c + 1) * 128], IDF[:])
                b.evac(MEMT[:, c, :], ps[:, 0:256])
        norm(lambda c: MEMT[:, c, :], lambda c: MEMN[:, c, :], 256, 16 + 8 * l, OND1)
        MKp, MVp = MKpL[l], MVpL[l]
        for h in range(4):
            ps = bank()
            for kc in range(8):
                b.mm(ps[:, 0:256], W[:, kc, h * 128:(h + 1) * 128], MEMN[:, kc, :], start=(kc == 0), stop=(kc == 7))
            b.evac(MKp[:, h, :], ps[:, 0:256])
        for mb in range(2):
            stg = STG[mb]
            for half in range(2):
                ps = bank()
                for kc in range(8):
                    b.mm(ps[:, :], MEMN[:, kc, mb * 128:(mb + 1) * 128], W[:, kc, half * 512:(half + 1) * 512],
                         start=(kc == 0), stop=(kc == 7))
                b.evac(stg[:, half * 512:(half + 1) * 512], ps[:, :])
            b.copy(MVp[:, mb, :], stg[:, 512:1024], eng="act")
            S.dma("sp", dap(memk_d, l * 256 * 512 + mb * 128 * 512, [(512, 128), (1, 512)]), stg[:, 0:512])
            S.dma("sp", dap(memv_d, l * 256 * 512 + mb * 128 * 512, [(512, 128), (1, 512)]), stg[:, 512:1024])

    def mem_kv_sample(l):
        S.dma("sp", CMS[:], dap(cmk_d, l * 256 * 512, [(512, 128), (128 * 512, 2), (1, 512)]))
        S.dma("pool", MVs[:], dap(cmv_d, l * 256 * 512, [(512, 128), (128 * 512, 2), (1, 512)]))
        for h in range(4):
            ps = bank()
            for mb in range(2):
                b.tr(ps[:, mb * 128:(mb + 1) * 128], CMS[:, mb, h * 128:(h + 1) * 128], IDF[:])
            b.evac(MKs[:, h, :], ps[:, 0:256])

    ebr = [0]

    def mem_attend(n, MK, MV):
        for h in range(4):
            es = []
            for mb in range(2):
                ps = bank()
                b.mm(ps[:, :n], MK[:, h, mb * 128:(mb + 1) * 128], QM[:, h, :n])
                e = SQ[sqr[0] % 3]
                sqr[0] += 1
                b.act(e[:, :n], ps[:, :n], ACTF.Exp, scale=128.0 ** -0.5)
                es.append(e)
            po = bank()
            pd = bank()
            for mb in range(2):
                b.mm(po[:, :n], MV[:, mb, h * 128:(h + 1) * 128], es[mb][:, :n], start=(mb == 0), stop=(mb == 1))
            for mb in range(2):
                b.mm(pd[:, :n], ONES[:], es[mb][:, :n], start=(mb == 0), stop=(mb == 1))
            b.act(TF[2][:, :n], pd[:, :n], ACTF.Ln)
            b.act(TF[2][:, :n], TF[2][:, :n], ACTF.Exp, scale=-1.0)
            b.tt(TMV[4 + h][:, :n], po[:, :n], TF[2][:, :n], ALU.mult)

    def proj(W, ocs, rhs, n, evac_fn):
        for oc in ocs:
            ps = bank()
            for kc in range(8):
                b.mm(ps[:, :n], W[:, kc, oc * 128:(oc + 1) * 128], rhs(kc), start=(kc == 0), stop=(kc == 7))
            evac_fn(oc, ps)

    TMV = [None] * 8

    def wout_residual(W, c0, n, mid=None):
        def ev(oc, ps):
            b.tt(XT[:, oc, c0:c0 + n], ps[:, :n], XT[:, oc, c0:c0 + n], ALU.add)
        proj(W, range(4), lambda kc: TMV[kc][:, :n], n, ev)
        if mid is not None:
            mid()
        proj(W, range(4, 8), lambda kc: TMV[kc][:, :n], n, ev)

    def mlp(l, after_y=None):
        def load_group(hg):
            w1 = WM1[hg % 2]
            w2 = WM2[hg % 2]
            for kc in range(8):
                S.dma("pool", w1[:, kc, :],
                      dap(w_ff1_d, l * D * 4 * D + kc * 128 * 4 * D + hg * 512, [(4 * D, 128), (1, 512)]))
            for hb in range(4):
                S.dma("pool", w2[:, hb, :],
                      dap(w_ff2_d, l * 4 * D * D + (hg * 512 + hb * 128) * D, [(D, 128), (1, D)]))
        MT = [(416 * i, 416) for i in range(5)]
        units = [(hg, t) for hg in range(8) for t in range(len(MT))]
        load_group(0)
        load_group(1)

        def emit_h(u, k):
            hg, t = u
            c0, n = MT[t]
            hq = HQ[k % 2]
            for hb in range(4):
                ps = bank()
                for kc in range(8):
                    b.mm(ps[:, :n], WM1[hg % 2][:, kc, hb * 128:(hb + 1) * 128], XN[:, kc, c0:c0 + n],
                         start=(kc == 0), stop=(kc == 7))
                rl = RL[hb % 2]
                b.act(rl[:, :n], ps[:, :n], ACTF.Relu)
                b.act(hq[:, hb, :n], rl[:, :n], ACTF.Square)

        def emit_y(u, k):
            hg, t = u
            c0, n = MT[t]
            hq = HQ[k % 2]
            for oc in range(8):
                ps = bank()
                for hb in range(4):
                    b.mm(ps[:, :n], WM2[hg % 2][:, hb, oc * 128:(oc + 1) * 128], hq[:, hb, :n],
                         start=(hb == 0), stop=(hb == 3))
                b.tt(XT[:, oc, c0:c0 + n], ps[:, :n], XT[:, oc, c0:c0 + n], ALU.add)

        for k, u in enumerate(units):
            emit_h(u, k)
            if k >= 1:
                emit_y(units[k - 1], k - 1)
                pu = units[k - 1]
                if pu[1] == len(MT) - 1 and pu[0] + 2 < 8:
                    load_group(pu[0] + 2)
                if after_y is not None and pu[0] == 7:
                    after_y(MT[pu[1]])
        emit_y(units[-1], len(units) - 1)
        if after_y is not None:
            after_y(MT[-1])

    mem_kv_prompt(0, True, WSQ[1])
    mem_kv_sample(0)
    load_sq(WSQ[1], w_out_d, 0)

    if STAGE < 1:
        return finish()
    norm(lambda c: XHT[:, c, :], lambda c: XNH[:, c, :], 256, 0, OND1)

    def ev_halo(oc, ps):
        b.evac(UH[:, oc, :, :], ps[:, 0:256].rearrange("p (b t) -> p b t", b=16))
    proj(WSQ[0], range(4), lambda kc: XNH[:, kc, :], 256, ev_halo)


    def pool_tile(t, c0, n):
        nb = 4 if n == 512 else 1
        tw = 144 if n == 512 else 48
        if n == 512:
            b.copy(U[:, :, :, 0:16], UH[:, :, 4 * t:4 * t + 4, :])
        Uv = U[:, :, 0:nb, 0:tw]
        PAv = PA[:, :, 0:nb, 0:tw]
        PBv = PB[:, :, 0:nb, 0:tw]
        b.tt(PAv[:, :, :, 1:tw], Uv[:, :, :, 1:tw], Uv[:, :, :, 0:tw - 1], ALU.add)
        b.tt(PBv[:, :, :, 3:tw], PAv[:, 1:4, :, 3:tw], PAv[:, 1:4, :, 1:tw - 2], ALU.add)
        def dd(g, srcv, w):
            b.stt(DD[:, g, 0:n].rearrange("p (b t) -> p b t", b=nb), srcv, 1.0 / w,
                  U[:, g, 0:nb, 16:tw], ALU.mult, ALU.subtract)
            if t == 0:
                b.tt(SMALL[:, g, :], srcv[:, 0, 0:16], RC16[:, g, :], ALU.mult)
                b.tt(DD[:, g, 0:16], SMALL[:, g, :], U[:, g, 0, 16:32], ALU.subtract)
        dd(0, PA[:, 0, 0:nb, 16:tw], 2)
        dd(1, PB[:, 0, 0:nb, 16:tw], 4)
        b.tt(PAv[:, 2:4, :, 7:tw], PBv[:, 1:3, :, 7:tw], PBv[:, 1:3, :, 3:tw - 4], ALU.add)
        dd(2, PA[:, 2, 0:nb, 16:tw], 8)
        b.tt(PBv[:, 0:1, :, 15:tw], PAv[:, 3:4, :, 15:tw], PAv[:, 3:4, :, 7:tw - 8], ALU.add)
        dd(3, PB[:, 0, 0:nb, 16:tw], 16)

    def pool_mm(n):
        for g in range(4):
            ps = bank()
            b.mm(ps[:, :n], WPOOL[:, g, :], DD[:, g, 0:n])
            b.act(TMV[g][:, :n], ps[:, :n], ACTF.Copy, scale=gcol(64 + g))

    def pool_out(slot, blk, col0):
        ps = bank()
        for g in range(4):
            b.tr(ps[0:15, g * 128:(g + 1) * 128], U[:, g, blk, col0:col0 + 15], IDF[:])
        b.evac(TF[3][0:15, :], ps[0:15, :])
        S.dma("sp", dap(poolo_d, slot * 15 * 512, [(512, 15), (1, 512)]), TF[3][0:15, :])

    XNT0 = [S.sbuf("XNT0_%d" % i, [128, 8, 512], BF16, at=WM + 8192 * i) for i in range(2)]
    TM0 = S.sbuf("TM0", [128, 8, 512], BF16, at=WM + 16384)
    for kc in range(8):
        TMV[kc] = TM0[:, kc, :]
    norm(lambda c: XT[:, c, 0:512], lambda c: XNT0[0][:, c, :512], 512, 0, OND1)
    for t, (c0, n) in enumerate(TILES):
        samp = n != 512
        xn = XNT0[t % 2]
        if samp:
            S.dma("sp", TF[3][0:15, :], dap(spool_d, 0, [(512, 15), (1, 512)]))
            ps = bank()
            for g in range(4):
                b.tr(ps[:, g * 16:g * 16 + 15], TF[3][0:15, g * 128:(g + 1) * 128], IDF[0:15, 0:15])
            b.evac(U[:, :, 0, 1:16], ps[:, 0:64].rearrange("p (g t) -> p g t", g=4)[:, :, 0:15])

        def ev0(oc, ps, n=n, samp=samp):
            if oc < 4:
                if not samp:
                    b.evac(U[:, oc, :, 16:144], ps[:, :].rearrange("p (b t) -> p b t", b=4))
                else:
                    b.evac(U[:, oc, 0, 16:48], ps[:, 0:32])
            else:
                b.evac(QM[:, oc - 4, :n], ps[:, :n])
        proj(WSQ[0], range(8), lambda kc: xn[:, kc, :n], n, ev0)
        if t + 1 < len(TILES):
            xload(t + 1)
        pool_tile(t, c0, n)
        mem_attend(n, MKs if samp else MKpL[0], MVs if samp else MVpL[0])
        pool_mm(n)
        mid0 = None
        if t + 1 < len(TILES):
            def mid0(t=t):
                c1, n1 = TILES[t + 1]
                xn1 = XNT0[(t + 1) % 2]
                norm(lambda c: XT[:, c, c1:c1 + n1], lambda c: xn1[:, c, :n1], n1, 0, OND1)
        if t == 3:
            pool_out(0, 3, 129)
        if samp:
            pool_out(1, 0, 33)
        wout_residual(WSQ[1], c0, n, mid=mid0)
    for kc in range(4):
        TMV[kc] = TMt1[:, kc, :]
        TMV[4 + kc] = TMm1[:, kc, :]

    if STAGE < 2:
        return finish()
    load_sq(WSQ[0], w_mem_d, D * D)
    mem_kv_prompt(1, True, WSQ[0])
    for t, (c0, n) in enumerate(TILES):
        norm(lambda c: XT[:, c, c0:c0 + n], lambda c: XN[:, c, c0:c0 + n], n, 32, OND1)
    load_sq(WSQ[0], w_kv_d, 0)
    load_sq(WSQ[1], w_in_d, D * D)
    def kv_norm(tile):
        c0, n = tile
        norm(lambda c: XT[:, c, c0:c0 + n], lambda c: XN[:, c, c0:c0 + n], n, 48, OND1)
    mlp(0, after_y=kv_norm)

    if STAGE < 3:
        return finish()
    mem_kv_sample(1)
    kq = [0]

    def exchange(pairs):
        for (ti, to) in pairs:
            S.collective("pool", lambda e, ti=ti, to=to: e.collective_compute(
                "AllGather", ALU.bypass, replica_groups=[[0, 1, 2, 3], [4, 5, 6, 7]],
                ins=[ti.ap().opt()], outs=[to.ap().opt()]), [ti.ap()], [to.ap()])

    def kv_kt(t):
        c0, n = TILES[t]
        samp = n != 512

        def evk(oc, ps):
            if samp:
                b.copy(KSN0[0:64, oc, :], ps[0:64, 0:32], eng="act")
                b.copy(KSN1[64:128, oc, :], ps[64:128, 0:32], eng="dve")
            else:
                ks = KST[kq[0] % 2]
                kq[0] += 1
                b.evac(ks[:, :], ps[:, :])
                S.dma("sp", dap(xk_in[t // 2], oc * 128 * 1024 + (c0 % 1024), [(1024, 128), (1, 512)]), ks[:, :])
        proj(WSQ[0], range(4), lambda kc: XN[:, kc, c0:c0 + n], n, evk)

    def kv_tok(tb):
        r0 = tb * 128
        rows = 128 if tb < 16 else 32
        stg = STG[tb % 2]
        for half in range(2):
            ps = bank()
            for kc in range(8):
                b.mm(ps[0:rows, :], XN[:, kc, r0:r0 + rows], WSQ[0][:, kc, half * 512:(half + 1) * 512],
                     start=(kc == 0), stop=(kc == 7))
            b.evac(stg[0:rows, half * 512:(half + 1) * 512], ps[0:rows, :])
        S.dma("sp", dap(ko_d, r0 * 512, [(512, rows), (1, 512)]), stg[0:rows, 0:512])
        S.dma("sp", dap(vo_d, r0 * 512, [(512, rows), (1, 512)]), stg[0:rows, 512:1024])
        if tb < 16:
            vs = VST[tb % 2]
            b.copy(vs[:, :], stg[:, 512:1024], eng="act")
            S.dma("sp", dap(xv_in[tb // 8], (r0 % 1024) * 512, [(512, 128), (1, 512)]), vs[:, :])
        else:
            b.copy(VSN[:, :], stg[0:32, 512:1024], eng="act")

    kv_kt(0)
    kv_kt(1)
    for tb in range(8):
        kv_tok(tb)
    exchange([(xk_in[0], xk_out[0]), (xv_in[0], xv_out[0])])
    kv_kt(2)
    kv_kt(3)
    kv_kt(4)
    for tb in range(8, 17):
        kv_tok(tb)
    load_sq(WSQ[0], w_out_d, D * D)
    exchange([(xk_in[1], xk_out[1]), (xv_in[1], xv_out[1])])

    if STAGE < 4:
        return finish()

    for i in range(NKBUF):
        b.memset(KB1[i][0:64, :], 0.0)
        S.dma("sp", KB0[i][64:67, :], dap(kaug_d, 0, [(512, 3), (1, 512)]))
        S.dma("sp", KB1[i][0:3, :], dap(kaug_d, 0, [(512, 3), (1, 512)]))
    b.memset(KSN1[0:64, :, :], 0.0)
    for h in range(4):
        S.dma("sp", KSN0[64:67, h, :], dap(kaug_d, 0, [(512, 3), (1, 32)]))
        S.dma("sp", KSN1[0:3, h, :], dap(kaug_d, 0, [(512, 3), (1, 32)]))
    b.memset(Q1[0:64, :, :], 0.0)

    kbr = [0]

    def finish_head(h, n, po0, po1, pd0, pd1, extra=(), lnbank=None):
        b.act(TF[2][:, :n], pd0, ACTF.Ln, extra=extra)
        b.act(TF[2][:, :n], TF[2][:, :n], ACTF.Exp, scale=-1.0)
        b.tt(OA[:, :n], po0, TF[2][:, :n], ALU.mult)
        b.act(TF[3][:, :n], pd1, ACTF.Ln)
        b.act(TF[3][:, :n], TF[3][:, :n], ACTF.Exp, scale=-1.0)
        b.tt(OB[:, :n], po1, TF[3][:, :n], ALU.mult)
        b.stt(OA[:, :n], OB[:, :n], NEGLAM, OA[:, :n], ALU.mult, ALU.add)
        sq = SQ[sqr[0] % 3]
        sqr[0] += 1
        b.act(sq[:, :n], OA[:, :n], ACTF.Square)
        ps = lnbank if lnbank is not None else bank()
        b.mm(ps[:, :n], OND2[:], sq[:, :n])
        b.act(TF[0][:, :n], ps[:, :n], ACTF.Ln, bias=EPS)
        b.act(TF[1][:, :n], TF[0][:, :n], ACTF.Exp, scale=-0.5)
        b.stt(TMV[h][:, :n], OA[:, :n], GSUB, TF[1][:, :n], ALU.mult, ALU.mult)

    def diff_attend_prompt(m):
        n = 512
        pending = [None]
        for h in range(4):
            items = []
            past = [(r, g) for g in range(m) for r in range(4)]
            diag = [(r, m) for r in range(4)]
            units = []
            kq_ = len(past) // 4
            for q in range(4):
                units += past[q * kq_:(q + 1) * kq_] + [diag[q]]
            SP = [PP[0], PP[1], PP[2]]
            PO0, PO1 = PS[6], PS[7]
            loaded = {}

            def load_unit(ui, h=h):
                r, g = units[ui]
                k = kbr[0] % NKBUF
                kbr[0] += 1
                kt = xk_out[g // 2]
                krow = r * 512 + h * 128
                S.dma("sp", KB0[k][0:64, :], dap(kt, krow * 1024 + 512 * (g % 2), [(1024, 64), (1, 512)]))
                S.dma("sp", KB1[k][64:128, :], dap(kt, (krow + 64) * 1024 + 512 * (g % 2), [(1024, 64), (1, 512)]))
                S.dma("sp", VB[k][:, :, :], dap(xv_out[g // 2], r * 256 * 2048 + 512 * (g % 2) * 512 + h * 128,
                                                 [(512, 128), (128 * 512, 4), (1, 128)]))
                loaded[ui] = k
            for ui, (r, g) in enumerate(units):
                for u in range(4):
                    kb = 4 * (4 * g + u) + r
                    if g < m:
                        items.append((ui, u, kb, 0, None))
                    else:
                        items.append((ui, u, kb, u * 128, r))
            nit = len(items)
            for ui in range(min(NKBUF - 1, len(units))):
                load_unit(ui)
            nxt = min(NKBUF - 1, len(units))
            hist = {}
            e0 = None
            for i in range(nit + 2):
                cur = None
                if i < nit:
                    ui, u, kb, lo, r = items[i]
                    k = loaded[ui]
                    bias = SLOPES[h] * 128.0 * kb
                    sp = SP[i % 3]
                    for c in range(2):
                        off = c * 512
                        KBc = KB0[k][0:67, u * 128:(u + 1) * 128] if c == 0 else KB1[k][:, u * 128:(u + 1) * 128]
                        Qc = Q0[0:67, h, lo:n] if c == 0 else Q1[:, h, lo:n]
                        b.mm(sp[:, off + lo:off + n], KBc, Qc, start=True, stop=(r is None))
                        if r is not None:
                            b.mm(sp[:, off + lo:off + lo + 128], IDB[:], DTAB[:, h, r, :], start=False, stop=True)
                    e = EBP[ebr[0] % NEB]
                    ebr[0] += 1
                    spv = sp[:, :].rearrange("p (c t) -> p c t", c=2)
                    b.act(e[:, :, lo:n], spv[:, :, lo:n], ACTF.Exp, bias=bias)
                    esb = ESP[ui % 2]
                    if u == 0:
                        e0 = e
                    elif u == 1:
                        if lo > 0:
                            b.copy(esb[:, :, 0:lo], e0[:, :, 0:lo], eng="dve")
                        b.tt(esb[:, :, lo:n], e0[:, :, lo:n], e[:, :, lo:n], ALU.add, eng="pool")
                    else:
                        b.tt(esb[:, :, lo:n], esb[:, :, lo:n], e[:, :, lo:n], ALU.add)
                        if u == 3 and ui % 2 == 1:
                            b.tt(esb[:], esb[:], ESP[(ui - 1) % 2][:], ALU.add)
                            if ui == 1:
                                b.copy(ACC[:], esb[:])
                            else:
                                b.tt(ACC[:], ACC[:], esb[:], ALU.add)
                    hist[i] = (k, u, lo, e, ui)
                if i - 2 >= 0:
                    pk, pu, plo, pe_, pui = hist.pop(i - 2)
                    first = (i - 2 == 0)
                    last = (i - 2 == nit - 1)
                    b.mm(PO0[:, plo:n], VB[pk][:, pu, :], pe_[:, 0, plo:n], start=first, stop=last)
                    b.mm(PO1[:, plo:n], VB[pk][:, pu, :], pe_[:, 1, plo:n], start=first, stop=last)
                if i < nit and items[i][1] == 2 and nxt < len(units):
                    load_unit(nxt)
                    nxt += 1
                if i == 6 and pending[0] is not None:
                    pending[0]()
                    pending[0] = None
            b.copy(OA[:, :n], PO0[:, :n], eng="act")
            b.copy(OB[:, :n], PO1[:, :n], eng="dve")

            b.copy(HI[:], ACC[:])
            b.tt(LO[:], ACC[:], HI[:], ALU.subtract)

            def epilogue(h=h):
                pd = [PS[4], PS[5]]
                for c in range(2):
                    b.mm(pd[c][:, :n], ONES[:], HI[:, c, :], start=True, stop=False)
                    b.mm(pd[c][:, :n], ONES[:], LO[:, c, :], start=False, stop=True)
                finish_head(h, n, OA[:, :n], OB[:, :n], pd[0][:, :n], pd[1][:, :n], lnbank=PS[4])
            if h < 3:
                pending[0] = epilogue
            else:
                epilogue()

    def diff_attend_sample():
        n = 32
        POD = PS[4]
        PT = [PS[0], PS[1]]
        PSS = [PS[2], PS[3]]
        EBS = [S.sbuf("EBS%d" % i, [128, 256], BF16, at=WM + 12288 + 512 * i) for i in range(4)]
        units = [(g, h) for g in range(4) for h in range(4)]
        kmap = {}

        def prep(ni):
            g, h = units[ni]
            if h == 0:
                S.dma("sp", CKS[:], dap(ck_d, g * 512 * 512, [(512, 128), (128 * 512, 4), (1, 512)]))
                S.dma("sp", CVSF[:], dap(cv_d, g * 512 * 512, [(512, 128), (128 * 512, 4), (1, 512)]))
            k = kbr[0] % NKBUF
            kbr[0] += 1
            kmap[ni] = k
            b.copy(VB[k][:, :, :], CVSF[:, :, h * 128:(h + 1) * 128], eng="dve")
            ps = PT[ni % 2]
            for u in range(4):
                b.tr(ps[:, u * 128:(u + 1) * 128], CKS[:, u, h * 128:(h + 1) * 128], IDF[:])
            b.copy(KB0[k][0:64, :], ps[0:64, :], eng="act")
            b.copy(KB1[k][64:128, :], ps[64:128, :], eng="dve")

        def scores(ni):
            g, h = units[ni]
            k = kmap[ni]
            ps = PSS[ni % 2]
            e = EBS[ni % 4]
            for u in range(4):
                for c in range(2):
                    o = (u * 2 + c) * 32
                    b.mm(ps[:, o:o + n], KB0[k][0:67, u * 128:(u + 1) * 128] if c == 0 else KB1[k][:, u * 128:(u + 1) * 128],
                         Q0[0:67, h, 0:n] if c == 0 else Q1[:, h, 0:n])
            for u in range(4):
                b.act(e[:, u * 64:(u + 1) * 64], ps[:, u * 64:(u + 1) * 64], ACTF.Exp,
                      bias=SLOPES[h] * 128.0 * (4 * g + u), extra=[ps[:, 0:256]])

        def av(ni):
            g, h = units[ni]
            k = kmap[ni]
            e = EBS[ni % 4]
            for u in range(4):
                for c in range(2):
                    o = (u * 2 + c) * 32
                    o0 = (h * 2 + c) * 32
                    b.mm(POD[:, o0:o0 + n], VB[k][:, u, :], e[:, o:o + n], start=False, stop=False, skip=True)
                    b.mm(POD[:, 256 + o0:256 + o0 + n], ONES[:], e[:, o:o + n], start=False, stop=False, skip=True)

        b.memset(POD[:, :], 0.0)
        prep(0)
        prep(1)
        for ni in range(len(units)):
            scores(ni)
            if ni + 2 < len(units):
                prep(ni + 2)
            av(ni)
        for h in range(4):
            ps = PSS[h % 2]
            e = EBS[h % 4]
            for c in range(2):
                o = c * 32
                b.mm(ps[0:32, o:o + n], KSN0[0:67, h, :] if c == 0 else KSN1[:, h, :],
                     Q0[0:67, h, 0:n] if c == 0 else Q1[:, h, 0:n], start=True, stop=False)
                b.mm(ps[0:32, o:o + n], IDB[0:32, 0:32], DTABS[:, h, :], start=False, stop=True)
            b.act(e[0:32, 0:64], ps[0:32, 0:64], ACTF.Exp, bias=SLOPES[h] * 128.0 * 16)
            for c in range(2):
                o = c * 32
                o0 = (h * 2 + c) * 32
                b.mm(POD[:, o0:o0 + n], VSN[0:32, h * 128:(h + 1) * 128], e[0:32, o:o + n], start=False, stop=True, skip=True)
                b.mm(POD[:, 256 + o0:256 + o0 + n], ONES[0:32, :], e[0:32, o:o + n], start=False, stop=True, skip=True)
        for h in range(4):
            o0 = (h * 2) * 32
            finish_head(h, n, POD[:, o0:o0 + n], POD[:, o0 + 32:o0 + 32 + n],
                        POD[:, 256 + o0:256 + o0 + n], POD[:, 256 + o0 + 32:256 + o0 + 32 + n], extra=[POD[:, :]])

    b3 = [0]

    def bank3():
        b3[0] += 1
        return PS[b3[0] % 4]

    order = [0, 1, 2, 3, 4]
    XNTL = [XNT, XNTB]
    c00, n00 = TILES[order[0]]
    norm(lambda c: XT[:, c, c00:c00 + n00], lambda c: XNTL[0][:, c, :n00], n00, 8, OND1)
    for idx, t in enumerate(order):
        c0, n = TILES[t]
        samp = n != 512
        xn = XNTL[idx % 2]
        S.dma("sp", Q0[64:67, :, 0:n], dap(qaug_d, c0, [(4 * NT, 3), (NT, 4), (1, n)]))
        S.dma("sp", Q1[0:3, :, 0:n], dap(qaug_d, c0, [(4 * NT, 3), (NT, 4), (1, n)]))

        def ev1(oc, ps, n=n):
            if oc < 4:
                b.act(Q0[0:64, oc, :n], ps[0:64, :n], ACTF.Copy, scale=0.125)
                b.act(Q1[64:128, oc, :n], ps[64:128, :n], ACTF.Copy, scale=0.125)
            else:
                b.evac(QM[:, oc - 4, :n], ps[:, :n])
        proj(WSQ[1], range(8), lambda kc: xn[:, kc, :n], n, ev1)
        mem_attend(n, MKs if samp else MKpL[1], MVs if samp else MVpL[1])
        if samp:
            diff_attend_sample()
        else:
            diff_attend_prompt(t)
        mid1 = None
        if idx + 1 < len(order):
            def mid1(idx=idx):
                c1, n1 = TILES[order[idx + 1]]
                xn1 = XNTL[(idx + 1) % 2]
                norm(lambda c: XT[:, c, c1:c1 + n1], lambda c: xn1[:, c, :n1], n1, 8, OND1)
        wout_residual(WSQ[0], c0, n, mid=mid1)

    if STAGE < 5:
        return finish()
    for t, (c0, n) in enumerate(TILES):
        norm(lambda c: XT[:, c, c0:c0 + n], lambda c: XN[:, c, c0:c0 + n], n, 40, OND1)
    mlp(1)
    if STAGE < 6:
        return finish()

    so = [0]
    YNL = [YN, S.sbuf("YNB", [128, 8, 512], F32, at=RW + 16384)]
    c00, n00 = TILES[0]
    norm(lambda c: XT[:, c, c00:c00 + n00], lambda c: YNL[0][:, c, :n00], n00, 56, OND1)
    for t, (c0, n) in enumerate(TILES):
        YN = YNL[t % 2]
        if t + 1 < len(TILES):
            c1, n1 = TILES[t + 1]
            yn1 = YNL[(t + 1) % 2]
            norm(lambda c: XT[:, c, c1:c1 + n1], lambda c: yn1[:, c, :n1], n1, 56, OND1)
        nb = 4 if n == 512 else 1
        for blk in range(nb):
            rows = 128 if n == 512 else 32
            stg = STG[so[0] % 2]
            so[0] += 1
            for half in range(2):
                ps = bank()
                for cc in range(4):
                    c = half * 4 + cc
                    b.tr(ps[0:rows, cc * 128:(cc + 1) * 128], YN[:, c, blk * 128:blk * 128 + rows], IDF[:])
                b.evac(stg[0:rows, half * 512:(half + 1) * 512], ps[0:rows, :])
            S.dma("sp", dap(y_d, (c0 + blk * 128) * D, [(D, rows), (1, D)]), stg[0:rows, :])

    return finish()


_PROG = {}
_STAGE = [99]


def _host_tables(j):
    pos = np.zeros(NT, np.int64)
    for i in range(16):
        gb = 4 * i + j
        pos[i * 128:(i + 1) * 128] = gb * 128 + np.arange(128)
    pos[NP:] = PAST + np.arange(NSM)
    qa = pos // 128
    qb = pos % 128
    qaug = np.zeros((3, 4, NT), np.float32)
    for h in range(4):
        qaug[0, h] = -SLOPES[h] * 128.0 * qa
        qaug[1, h] = -SLOPES[h] * qb
        qaug[2, h] = SLOPES[h]
    kaug = np.zeros((3, 512), np.float32)
    kaug[0] = 1.0
    kaug[1] = 1.0
    kaug[2] = np.arange(512) % 128
    ii = np.arange(128)[:, None]
    qq = np.arange(128)[None, :]
    dtab = np.zeros((128, 4, 4, 128), np.float32)
    for h in range(4):
        dd = np.where(ii // 64 > qq // 64, NEG, np.where(ii > qq, -2.0 * SLOPES[h] * (ii - qq), 0.0))
        for r in range(4):
            if r < j:
                dtab[:, h, r, :] = 0.0
            elif r == j:
                dtab[:, h, r, :] = dd
            else:
                dtab[:, h, r, :] = NEG
    i2 = np.arange(32)[:, None]
    q2 = np.arange(32)[None, :]
    dtabs = np.zeros((32, 4, 32), np.float32)
    for h in range(4):
        dtabs[:, h, :] = np.where(i2 > q2, -2.0 * SLOPES[h] * (i2 - q2), 0.0)
    rc16 = np.zeros((4, 16), np.float32)
    for g, w in enumerate((2, 4, 8, 16)):
        if j == 0:
            rc16[g] = 1.0 / np.minimum(np.arange(16) + 1, w)
        else:
            rc16[g] = 1.0 / w
    import ml_dtypes
    qaug = qaug.astype(ml_dtypes.bfloat16)
    kaug = kaug.astype(ml_dtypes.bfloat16)
    return dict(qaug=qaug, kaug=kaug, dtab=dtab.reshape(128, -1), dtabs=dtabs.reshape(32, -1),
                rc16=rc16.reshape(1, 64), identd=np.eye(128, dtype=np.float32))


def kernel(x_prompt, x_sample, mem_prompt, cache_k, cache_v, cache_mem_k, cache_mem_v, state_pool,
           g_attn, w_in, w_out, g_mem, w_mem_kv, g_ffn, w_ff1, w_ff2, w_pool, pool_scale,
           lambda_qk, g_subln, g_kv, w_kv, g_final):
    f = lambda a: np.ascontiguousarray(np.asarray(a, dtype=np.float32))
    x_prompt, x_sample, mem_prompt = f(x_prompt), f(x_sample), f(mem_prompt)
    cache_k, cache_v, cache_mem_k, cache_mem_v = f(cache_k), f(cache_v), f(cache_mem_k), f(cache_mem_v)
    state_pool = f(state_pool)
    if "p" not in _PROG:
        _PROG["p"] = build_program(_STAGE[0])
    nc = _PROG["p"][0]

    def colv(v):
        return f(v).reshape(8, 128).T
    cols = np.concatenate([colv(g_attn[0]), colv(g_attn[1]), colv(g_mem[0]), colv(g_mem[1]),
                           colv(g_ffn[0]), colv(g_ffn[1]), colv(g_kv), colv(g_final),
                           f(pool_scale).reshape(4, 128).T, f(g_subln).reshape(1, 128).T], axis=1)
    shared = dict(w_in=f(w_in), w_out=f(w_out), w_mem_kv=f(w_mem_kv), w_ff1=f(w_ff1), w_ff2=f(w_ff2),
                  w_pool=f(w_pool).reshape(4, 128, 128), w_kv=f(w_kv), lq=f(lambda_qk).reshape(1, 256),
                  cols=np.ascontiguousarray(cols.astype(np.float32)))
    in_maps = []
    for c in range(8):
        bq, j = c // 4, c % 4
        xp = x_prompt[bq].reshape(64, 128, D)
        blocks = xp[j::4]
        xx = np.concatenate([blocks.reshape(NP, D), x_sample[c]], axis=0)
        xh = np.zeros((16, 16, D), np.float32)
        for i in range(16):
            gb = 4 * i + j
            if gb > 0:
                xh[i] = x_prompt[bq, gb * 128 - 16:gb * 128]
        m = dict(shared)
        m.update(x=np.ascontiguousarray(xx), xh=xh.reshape(NH, D), mem=mem_prompt[bq],
                 ck=cache_k[c].reshape(PAST, 512), cv=cache_v[c].reshape(PAST, 512),
                 cmk=np.ascontiguousarray(cache_mem_k[:, c].reshape(2, 256, 512)),
                 cmv=np.ascontiguousarray(cache_mem_v[:, c].reshape(2, 256, 512)),
                 spool=state_pool[0, c])
        m.update(_host_tables(j))
        in_maps.append(m)
    res = run_bass_kernel_spmd(nc, in_maps, core_ids=list(range(8)))
    R = res.results
    y_prompt = np.zeros((2, SEQ, D), np.float32)
    k_prompt = np.zeros((2, SEQ, 4, 128), np.float32)
    v_prompt = np.zeros((2, SEQ, 4, 128), np.float32)
    y_sample = np.zeros((8, NSM, D), np.float32)
    k_sample = np.zeros((8, NSM, 4, 128), np.float32)
    v_sample = np.zeros((8, NSM, 4, 128), np.float32)
    pool_sample = np.zeros((1, 8, 15, 512), np.float32)
    for c in range(8):
        bq, j = c // 4, c % 4
        r = R[c]
        yv = y_prompt[bq].reshape(64, 128, D)
        kv = k_prompt[bq].reshape(64, 128, 512)
        vv = v_prompt[bq].reshape(64, 128, 512)
        yv[j::4] = r["y"][:NP].reshape(16, 128, D)
        kv[j::4] = r["ko"][:NP].reshape(16, 128, 512)
        vv[j::4] = r["vo"][:NP].reshape(16, 128, 512)
        y_sample[c] = r["y"][NP:]
        k_sample[c] = r["ko"][NP:].reshape(NSM, 4, 128)
        v_sample[c] = r["vo"][NP:].reshape(NSM, 4, 128)
        pool_sample[0, c] = r["poolo"][1]
    mem_k_prompt = np.stack([R[0]["memk"], R[4]["memk"]], axis=1).reshape(2, 2, 256, 4, 128)
    mem_v_prompt = np.stack([R[0]["memv"], R[4]["memv"]], axis=1).reshape(2, 2, 256, 4, 128)
    pool_prompt = np.stack([R[3]["poolo"][0], R[7]["poolo"][0]], axis=0)[None]
    return (y_prompt, y_sample, np.ascontiguousarray(mem_k_prompt), np.ascontiguousarray(mem_v_prompt),
            np.ascontiguousarray(pool_prompt), k_prompt, v_prompt, pool_sample, k_sample, v_sample)
```
